# Optimizing a Trainium2 kernel written in Bass

```python
import math
import jax, jax.numpy as jnp
from jax import lax
import numpy as np

D_MODEL = 1024
BATCH = 8
SEQ = 8192
DEPTH = 4
DEC_BATCH = 8
DEC_SEQ = 16
PAST_LEN = 4096

CHUNK = 64
D_FF = 2816
EPS = 1e-6
POOL_W = D_MODEL // 2
POOL_GROUPS = 4
POOL_GC = POOL_W // POOL_GROUPS
POOL_WINDOWS = (2, 4, 8, 16)
POOL_HIST = 15
SSM_INNER = D_MODEL
SSM_HEADDIM = 64
SSM_HEADS = SSM_INNER // SSM_HEADDIM
SSM_GROUPS = 4
SSM_STATE = 128
SSM_CONV = 4
XBC_W = SSM_INNER + 2 * SSM_GROUPS * SSM_STATE
GMLP_W = D_MODEL // 2
GMLP_HEADS = 4
GMLP_HC = GMLP_W // GMLP_HEADS
GMLP_CHUNK = 128
N_BRANCH = 3
IN_COLS = POOL_W + SSM_INNER + XBC_W + SSM_HEADS + 2 * GMLP_W + N_BRANCH * D_MODEL

kernel_name = 'hybrid_streaming_encoder_step'


def rms_norm(x, g):
    xf = x.astype(jnp.float32)
    y = xf * lax.rsqrt(jnp.mean(xf * xf, axis=-1, keepdims=True) + EPS)
    return (y * g.astype(jnp.float32)).astype(x.dtype)


def layer_norm(x, g, b):
    xf = x.astype(jnp.float32)
    mu = jnp.mean(xf, axis=-1, keepdims=True)
    xc = xf - mu
    y = xc * lax.rsqrt(jnp.mean(xc * xc, axis=-1, keepdims=True) + EPS)
    return (y * g.astype(jnp.float32) + b.astype(jnp.float32)).astype(x.dtype)


def swiglu(x, w_gu, w_down):
    gate, up = jnp.split(x @ w_gu, 2, axis=-1)
    return (jax.nn.silu(gate) * up) @ w_down


def causal_dwconv(x, hist, w, b):
    L = x.shape[1]
    xp = jnp.concatenate([hist, x], axis=1)
    out = b + w[0] * xp[:, 0:L]
    for k in range(1, SSM_CONV):
        out = out + w[k] * xp[:, k:k + L]
    return out, xp[:, L:]


def pool_mixer(xa, hist, pos0, pool_w, pool_scale):
    b, L, C = xa.shape
    ext = jnp.concatenate([hist, xa], axis=1)
    cs = jnp.cumsum(ext.astype(jnp.float32), axis=1)
    cs = jnp.concatenate([jnp.zeros_like(cs[:, :1]), cs], axis=1)
    off = POOL_HIST + 1
    pos = pos0 + jnp.arange(L)
    means = []
    for gi, w in enumerate(POOL_WINDOWS):
        sl = slice(gi * POOL_GC, (gi + 1) * POOL_GC)
        s = cs[:, off:, sl] - cs[:, off - w:off - w + L, sl]
        cnt = jnp.minimum(pos + 1, w).astype(jnp.float32)[None, :, None]
        means.append(s / cnt)
    mean = jnp.concatenate(means, axis=-1).astype(xa.dtype)
    zz = (mean - xa).reshape(b, L, POOL_GROUPS, POOL_GC)
    y = jnp.einsum('blgc,gcd->blgd', zz, pool_w).reshape(b, L, C) * pool_scale
    return y, ext[:, L:]


def ssd_scan(x, dt, a, bm, cm, s0, blk):
    b, L, H, P = x.shape
    G, N = bm.shape[2], bm.shape[3]
    R = H // G
    nc = L // blk
    f32 = jnp.float32
    xf = x.astype(f32).reshape(b, nc, blk, G, R, P)
    dtf = dt.astype(f32).reshape(b, nc, blk, G, R)
    bf = bm.astype(f32).reshape(b, nc, blk, G, N)
    cf = cm.astype(f32).reshape(b, nc, blk, G, N)
    cum = jnp.cumsum(dtf * a.reshape(G, R), axis=2)
    cum_t = jnp.moveaxis(cum, 2, -1)
    mask = jnp.tril(jnp.ones((blk, blk), dtype=bool))
    decay = jnp.exp(jnp.where(mask, cum_t[..., :, None] - cum_t[..., None, :], -jnp.inf))
    cb = jnp.einsum('bclgn,bcsgn->bcgls', cf, bf)
    mmat = cb[:, :, :, None] * decay * jnp.moveaxis(dtf, 2, -1)[..., None, :]
    y_diag = jnp.einsum('bcgrls,bcsgrp->bclgrp', mmat, xf)
    last = cum[:, :, -1]
    xw = xf * (jnp.exp(last[:, :, None] - cum) * dtf)[..., None]
    chunk_states = jnp.einsum('bclgn,bclgrp->bcgrpn', bf, xw)

    def step(s, inp):
        dec, st = inp
        return s * dec[..., None, None] + st, s

    s_fin, s_in = lax.scan(step, s0.astype(f32).reshape(b, G, R, P, N),
                           (jnp.moveaxis(jnp.exp(last), 1, 0), jnp.moveaxis(chunk_states, 1, 0)))
    s_in = jnp.moveaxis(s_in, 0, 1)
    y_off = jnp.einsum('bclgn,bcgrpn->bclgrp', cf, s_in) * jnp.exp(cum)[..., None]
    y = (y_diag + y_off).reshape(b, L, H, P).astype(x.dtype)
    return y, s_fin.reshape(b, H, P, N).astype(s0.dtype)


def spatial_gate(u, vn, ws, bs):
    b, L, C = vn.shape
    blk = min(L, GMLP_CHUNK)
    n = L // blk
    mask = jnp.tril(jnp.ones((blk, blk), dtype=bool))
    wsm = jnp.where(mask, ws[:, :blk, :blk], 0.0).astype(vn.dtype)
    vh = vn.reshape(b, n, blk, GMLP_HEADS, GMLP_HC)
    sv = jnp.einsum('hts,bnshc->bnthc', wsm, vh) + bs[:, :blk].T[:, :, None].astype(vn.dtype)
    return u * sv.reshape(b, L, C)


def mixer(xn, hist_pool, hist_conv, s0, pos0, w_in, pool_w, pool_scale, conv_w, conv_b, dt_bias,
          a_log, d_skip, norm_g, gn_g, gn_b, ws, bs, wa, wb, wc, wo):
    b, L, _ = xn.shape
    proj = xn @ w_in
    cuts = np.cumsum([POOL_W, SSM_INNER, XBC_W, SSM_HEADS, GMLP_W, GMLP_W]).tolist()
    xa, z, xbc, dt_raw, u, v, gate_cols = jnp.split(proj, cuts, axis=-1)
    ya, new_pool = pool_mixer(xa, hist_pool, pos0, pool_w, pool_scale)
    xbc, new_conv = causal_dwconv(xbc, hist_conv, conv_w, conv_b)
    xbc = jax.nn.silu(xbc)
    xs, bm, cm = jnp.split(xbc, [SSM_INNER, SSM_INNER + SSM_GROUPS * SSM_STATE], axis=-1)
    xs = xs.reshape(b, L, SSM_HEADS, SSM_HEADDIM)
    bm = bm.reshape(b, L, SSM_GROUPS, SSM_STATE)
    cm = cm.reshape(b, L, SSM_GROUPS, SSM_STATE)
    dt = jax.nn.softplus(dt_raw.astype(jnp.float32) + dt_bias.astype(jnp.float32))
    a = -jnp.exp(a_log.astype(jnp.float32))
    y, new_ssm = ssd_scan(xs, dt, a, bm, cm, s0, min(L, CHUNK))
    y = y + d_skip[:, None].astype(y.dtype) * xs
    yb = rms_norm(y.reshape(b, L, SSM_INNER) * jax.nn.silu(z), norm_g)
    vn = layer_norm(v, gn_g, gn_b)
    yc = spatial_gate(u, vn, ws, bs)
    gates = jax.nn.sigmoid(gate_cols.astype(jnp.float32)).astype(xn.dtype).reshape(b, L, N_BRANCH, D_MODEL)
    merged = gates[:, :, 0] * (ya @ wa) + gates[:, :, 1] * (yb @ wb) + gates[:, :, 2] * (yc @ wc)
    return merged @ wo, new_pool, new_conv, new_ssm, vn


def run_trunk(x, pool_st, conv_st, ssm_st, pos0,
              ffn1_pre_g, ffn1_post_g, ffn1_w_gu, ffn1_w_down,
              mix_pre_g, mix_post_g, w_in, pool_w, pool_scale,
              ssm_conv_w, ssm_conv_b, ssm_dt_bias, ssm_a_log, ssm_d, ssm_norm_g,
              gmlp_norm_g, gmlp_norm_b, gmlp_ws, gmlp_bs,
              w_branch_a, w_branch_b, w_branch_c, w_out,
              ffn2_pre_g, ffn2_post_g, ffn2_w_gu, ffn2_w_down):
    new_pool, new_conv, new_ssm, new_v = [], [], [], []
    h = x
    for l in range(DEPTH):
        f = swiglu(rms_norm(h, ffn1_pre_g[l]), ffn1_w_gu[l], ffn1_w_down[l])
        h = h + 0.5 * rms_norm(f, ffn1_post_g[l])
        m, ps, cs, ss, vn = mixer(rms_norm(h, mix_pre_g[l]), pool_st[l], conv_st[l], ssm_st[l], pos0,
                                  w_in[l], pool_w[l], pool_scale[l], ssm_conv_w[l], ssm_conv_b[l],
                                  ssm_dt_bias[l], ssm_a_log[l], ssm_d[l], ssm_norm_g[l],
                                  gmlp_norm_g[l], gmlp_norm_b[l], gmlp_ws[l], gmlp_bs[l],
                                  w_branch_a[l], w_branch_b[l], w_branch_c[l], w_out[l])
        h = h + rms_norm(m, mix_post_g[l])
        f = swiglu(rms_norm(h, ffn2_pre_g[l]), ffn2_w_gu[l], ffn2_w_down[l])
        h = h + 0.5 * rms_norm(f, ffn2_post_g[l])
        new_pool.append(ps)
        new_conv.append(cs)
        new_ssm.append(ss)
        new_v.append(vn)
    return h, jnp.stack(new_pool), jnp.stack(new_conv), jnp.stack(new_ssm), new_v


def setup_inputs(seed: int = 0) -> dict:
    key = jax.random.key(seed)
    ks = jax.random.split(key, 32)
    f32 = jnp.float32

    def nrm(k, shape, scale):
        return scale * jax.random.normal(k, shape, f32)

    def gain(k, shape):
        return 1.0 + 0.05 * jax.random.normal(k, shape, f32)

    dt0 = jnp.exp(jax.random.uniform(ks[16], (DEPTH, SSM_HEADS), f32, math.log(1e-3), math.log(1e-1)))
    return {
        'x_prompt': nrm(ks[0], (BATCH, SEQ, D_MODEL), 1.0),
        'x_sample': nrm(ks[1], (DEC_BATCH, DEC_SEQ, D_MODEL), 1.0),
        'state_pool': nrm(ks[2], (DEPTH, DEC_BATCH, POOL_HIST, POOL_W), 1.0),
        'state_conv': nrm(ks[3], (DEPTH, DEC_BATCH, SSM_CONV - 1, XBC_W), 1.0),
        'state_ssm': nrm(ks[4], (DEPTH, DEC_BATCH, SSM_HEADS, SSM_HEADDIM, SSM_STATE), 0.5),
        'ffn1_pre_g': gain(ks[5], (DEPTH, D_MODEL)),
        'ffn1_post_g': gain(ks[6], (DEPTH, D_MODEL)),
        'ffn1_w_gu': nrm(ks[7], (DEPTH, D_MODEL, 2 * D_FF), D_MODEL ** -0.5),
        'ffn1_w_down': nrm(ks[8], (DEPTH, D_FF, D_MODEL), D_FF ** -0.5),
        'mix_pre_g': gain(ks[9], (DEPTH, D_MODEL)),
        'mix_post_g': gain(ks[10], (DEPTH, D_MODEL)),
        'w_in': nrm(ks[11], (DEPTH, D_MODEL, IN_COLS), D_MODEL ** -0.5),
        'pool_w': nrm(ks[12], (DEPTH, POOL_GROUPS, POOL_GC, POOL_GC), POOL_GC ** -0.5),
        'pool_scale': gain(ks[13], (DEPTH, POOL_W)),
        'ssm_conv_w': nrm(ks[14], (DEPTH, SSM_CONV, XBC_W), SSM_CONV ** -0.5),
        'ssm_conv_b': nrm(ks[15], (DEPTH, XBC_W), 0.01),
        'ssm_dt_bias': dt0 + jnp.log(-jnp.expm1(-dt0)),
        'ssm_a_log': jnp.log(jax.random.uniform(ks[17], (DEPTH, SSM_HEADS), f32, 1.0, 16.0)),
        'ssm_d': gain(ks[18], (DEPTH, SSM_HEADS)),
        'ssm_norm_g': gain(ks[19], (DEPTH, SSM_INNER)),
        'gmlp_norm_g': gain(ks[20], (DEPTH, GMLP_W)),
        'gmlp_norm_b': nrm(ks[21], (DEPTH, GMLP_W), 0.01),
        'gmlp_ws': nrm(ks[22], (DEPTH, GMLP_HEADS, GMLP_CHUNK, GMLP_CHUNK), GMLP_CHUNK ** -0.5),
        'gmlp_bs': gain(ks[23], (DEPTH, GMLP_HEADS, GMLP_CHUNK)),
        'w_branch_a': nrm(ks[24], (DEPTH, POOL_W, D_MODEL), POOL_W ** -0.5),
        'w_branch_b': nrm(ks[25], (DEPTH, SSM_INNER, D_MODEL), SSM_INNER ** -0.5),
        'w_branch_c': nrm(ks[26], (DEPTH, GMLP_W, D_MODEL), GMLP_W ** -0.5),
        'w_out': nrm(ks[27], (DEPTH, D_MODEL, D_MODEL), D_MODEL ** -0.5),
        'ffn2_pre_g': gain(ks[28], (DEPTH, D_MODEL)),
        'ffn2_post_g': gain(ks[29], (DEPTH, D_MODEL)),
        'ffn2_w_gu': nrm(ks[30], (DEPTH, D_MODEL, 2 * D_FF), D_MODEL ** -0.5),
        'ffn2_w_down': nrm(ks[31], (DEPTH, D_FF, D_MODEL), D_FF ** -0.5),
    }


def reference(x_prompt, x_sample, state_pool, state_conv, state_ssm,
              ffn1_pre_g, ffn1_post_g, ffn1_w_gu, ffn1_w_down,
              mix_pre_g, mix_post_g, w_in, pool_w, pool_scale,
              ssm_conv_w, ssm_conv_b, ssm_dt_bias, ssm_a_log, ssm_d, ssm_norm_g,
              gmlp_norm_g, gmlp_norm_b, gmlp_ws, gmlp_bs,
              w_branch_a, w_branch_b, w_branch_c, w_out,
              ffn2_pre_g, ffn2_post_g, ffn2_w_gu, ffn2_w_down):
    weights = (ffn1_pre_g, ffn1_post_g, ffn1_w_gu, ffn1_w_down,
               mix_pre_g, mix_post_g, w_in, pool_w, pool_scale,
               ssm_conv_w, ssm_conv_b, ssm_dt_bias, ssm_a_log, ssm_d, ssm_norm_g,
               gmlp_norm_g, gmlp_norm_b, gmlp_ws, gmlp_bs,
               w_branch_a, w_branch_b, w_branch_c, w_out,
               ffn2_pre_g, ffn2_post_g, ffn2_w_gu, ffn2_w_down)
    bp = x_prompt.shape[0]
    zero_pool = jnp.zeros((DEPTH, bp, POOL_HIST, POOL_W), x_prompt.dtype)
    zero_conv = jnp.zeros((DEPTH, bp, SSM_CONV - 1, XBC_W), x_prompt.dtype)
    zero_ssm = jnp.zeros((DEPTH, bp, SSM_HEADS, SSM_HEADDIM, SSM_STATE), x_prompt.dtype)
    y_prompt, pool_p, conv_p, ssm_p, _ = run_trunk(x_prompt, zero_pool, zero_conv, zero_ssm, 0, *weights)
    y_sample, pool_s, conv_s, ssm_s, v_rows = run_trunk(x_sample, state_pool, state_conv, state_ssm, PAST_LEN, *weights)
    gmlp_v_s = jnp.stack(v_rows)
    return (y_prompt, y_sample, pool_p, conv_p, ssm_p, pool_s, conv_s, ssm_s, gmlp_v_s)
```

```python
import numpy as np
import concourse.bass as bass
import concourse.mybir as mybir
from concourse.bass_utils import run_bass_kernel_spmd

F32, BF16 = mybir.dt.float32, mybir.dt.bfloat16
AF = mybir.ActivationFunctionType
ALU = mybir.AluOpType

D = 1024
DFF = 2816
INC = 7696
EPS = 1e-6
GR = 256
NDS = 8
NV = 1732


def _es(dt):
    return 2 if dt == BF16 else 4


def rng(ap):
    t = ap.tensor
    es = _es(ap.dtype)
    shp = list(t.shape)
    rowb = int(np.prod(shp[1:])) * _es(t.dtype)
    f0 = (ap.offset * es) % rowb
    ext = 1
    for st, c in list(ap.ap)[1:]:
        ext += (c - 1) * abs(st)
    if type(t).__name__.startswith("PSum"):
        return "ps", f0, f0 + ext * es, 2048
    base = t.manual_sbuf_range[0]
    return "sb", base + f0, base + f0 + ext * es, GR


class Prog:
    ENG = ["pe", "act", "dve", "pool", "sp"]

    def __init__(s):
        s.q = {e: [] for e in s.ENG}
        s.cnt = {e: 0 for e in s.ENG}
        s.waited = {e: {} for e in s.ENG}
        s.lw = {}
        s.rd = {}
        s.dma_tot = [0] * NDS
        s.dma_rr = 0
        s.tag = ''
        s.tags = {e: [] for e in s.ENG}

    def keys(s, x):
        if isinstance(x, tuple):
            return [x]
        sp, lo, hi, g = rng(x)
        return [(sp, i) for i in range(lo // g, (hi - 1) // g + 1)]

    def _deps(s, eng, reads, writes):
        need = {}

        def add(sk, v):
            if need.get(sk, 0) < v:
                need[sk] = v
        for r in reads:
            for k in s.keys(r):
                if k in s.lw:
                    add(*s.lw[k])
        for w in writes:
            for k in s.keys(w):
                if k in s.lw:
                    add(*s.lw[k])
                for sk, v in s.rd.get(k, {}).items():
                    add(sk, v)
        waits = []
        for sk, v in need.items():
            if sk == eng and eng == "pe":
                continue
            if s.waited[eng].get(sk, 0) >= v:
                continue
            s.waited[eng][sk] = v
            waits.append((sk, v))
        return waits

    def _commit(s, ev, reads, writes):
        for r in reads:
            for k in s.keys(r):
                d = s.rd.setdefault(k, {})
                if d.get(ev[0], 0) < ev[1]:
                    d[ev[0]] = ev[1]
        for w in writes:
            for k in s.keys(w):
                s.lw[k] = ev
                s.rd[k] = {}

    def op(s, eng, fn, reads, writes):
        waits = s._deps(eng, reads, writes)
        s.cnt[eng] += 1
        s.q[eng].append((waits, fn, eng, 1))
        s.tags[eng].append(s.tag)
        s._commit((eng, s.cnt[eng]), reads, writes)

    def dma(s, qeng, fn, reads, writes):
        waits = s._deps(qeng, reads, writes)
        i = s.dma_rr
        s.dma_rr = (i + 1) % NDS
        sk = "d%d" % i
        if s.dma_tot[i] > 0 and s.waited[qeng].get(sk, 0) < s.dma_tot[i]:
            waits.append((sk, s.dma_tot[i]))
            s.waited[qeng][sk] = s.dma_tot[i]
        s.dma_tot[i] += 16
        s.q[qeng].append((waits, fn, sk, 16))
        s.tags[qeng].append(s.tag)
        s._commit((sk, s.dma_tot[i]), reads, writes)


def build(NT, DEPTH, has_sample=True):
    nc = bass.Bass("TRN2", target_bir_lowering=False)
    P = Prog()

    def din(name, shape, dt=F32):
        return nc.dram_tensor(name, list(shape), dt, kind="ExternalInput").ap()

    def dout(name, shape):
        return nc.dram_tensor(name, list(shape), F32, kind="ExternalOutput").ap()

    def dscr(name, shape):
        return nc.dram_tensor(name, list(shape), BF16, kind="Internal").ap()

    xT = din("xT", [NT, 128, 8, 512])
    xsT = din("xsT", [128, 8, 16])
    pool_st = din("pool_st", [DEPTH, 15, 512])
    conv_st = din("conv_st", [DEPTH, 128, 16, 3])
    ssm_st = din("ssm_st", [DEPTH, 128, 1024])
    w_gu = [din("w_gu1", [DEPTH, D, 2 * DFF]), din("w_gu2", [DEPTH, D, 2 * DFF])]
    w_dn = [din("w_dn1", [DEPTH, DFF, D]), din("w_dn2", [DEPTH, DFF, D])]
    w_in = din("w_in", [DEPTH, D, INC])
    w_a = din("w_a", [DEPTH, 512, D])
    w_b = din("w_b", [DEPTH, D, D])
    w_c = din("w_c", [DEPTH, 512, D])
    w_o = din("w_o", [DEPTH, D, D])
    vecs = din("vecs", [DEPTH, 128, NV])
    pool_w = din("pool_w", [DEPTH, 4, 128, 128])
    ws_in = din("ws", [DEPTH, 4, 128, 128])
    cst = din("cst", [128, 17, 128])

    yT = dout("yT", [NT, 128, 8, 512])
    ysT = dout("ysT", [128, 8, 16])
    o_pool = [dout("npool_p", [DEPTH, 15, 512]), dout("npool_s", [DEPTH, 15, 512])]
    o_conv = [dout("nconv_p", [DEPTH, 128, 16, 3]), dout("nconv_s", [DEPTH, 128, 16, 3])]
    o_ssm = [dout("nssm_p", [DEPTH, 128, 1024]), dout("nssm_s", [DEPTH, 128, 1024])]
    o_gv = dout("gv_s", [DEPTH, 16, 512])

    gu_s = [dscr("gu1_s", [DEPTH, 22, 128, 8, 256]), dscr("gu2_s", [DEPTH, 22, 128, 8, 256])]
    dn_s = [dscr("dn1_s", [DEPTH, 8, 128, 22, 128]), dscr("dn2_s", [DEPTH, 8, 128, 22, 128])]
    wtm_s = dscr("wtm_s", [DEPTH, 4, 128, 2, 1040])
    wfm_s = dscr("wfm_s", [DEPTH, 14, 128, 8, 256])
    wmg_s = dscr("wmg_s", [DEPTH, 8, 128, 40, 128])
    wo_s = dscr("wo_s", [DEPTH, 2, 128, 4, 8, 128])

    SB0 = 20480
    cur = [SB0]

    def sb(name, shape, dt, at=None):
        nb = int(np.prod(shape[1:])) * _es(dt)
        nb = (nb + GR - 1) // GR * GR
        if at is None:
            off = cur[0]
            cur[0] += nb
        else:
            off = at
        return nc.alloc_sbuf_tensor_at(name, list(shape), dt, offset=off), off, nb

    h, _, _ = sb("h", [128, 8, 512], F32)
    S32 = [sb("S32_%d" % l, [128, 1024], F32)[0] for l in range(DEPTH)]
    Sbf1, _, _ = sb("Sbf1", [128, 1024], BF16)
    hist = [sb("hist_%d" % l, [128, 16, 3], F32)[0] for l in range(DEPTH)]
    xac = [sb("xac_%d" % l, [128, 512], BF16)[0] for l in range(DEPTH)]
    cf, _, _ = sb("cf", [128, 4, 128], F32)
    cb, _, _ = sb("cb", [128, 15, 128], BF16)
    wsmT, _, _ = sb("wsmT", [128, DEPTH * 4, 128], BF16)
    plw, _, _ = sb("plw", [128, DEPTH * 4, 128], BF16)
    vecG = [sb("vecG%d" % i, [128, 56], F32)[0] for i in range(2)]
    vecM, _, _ = sb("vecM", [128, NV - 56], F32)
    NSLOT = 6
    SLOTN = 2816
    slots = [sb("slot%d" % i, [128, SLOTN], BF16)[0] for i in range(NSLOT)]
    A0 = cur[0]
    xn, _, _ = sb("xn", [128, 8, 512], BF16)
    rstd, _, _ = sb("rstd", [128, 512], F32)
    tA, _, _ = sb("tA", [128, 512], F32)
    tB, _, _ = sb("tB", [128, 512], F32)
    tC, _, _ = sb("tC", [128, 512], F32)
    RA = cur[0]
    hid, _, _ = sb("hid", [128, 22, 512], BF16)
    fsb, _, _ = sb("fsb", [128, 8, 512], F32)
    RAend = cur[0]
    cur[0] = RA
    sz, _, _ = sb("sz", [128, 8, 512], BF16)
    xs, _, _ = sb("xs", [128, 8, 512], BF16)
    BC, _, _ = sb("BC", [128, 8, 512], BF16)
    yb, _, _ = sb("yb", [128, 8, 512], BF16)
    ya, _, _ = sb("ya", [128, 4, 512], BF16)
    assert cur[0] <= RAend
    cur[0] = RAend
    yc, _, _ = sb("yc", [128, 4, 512], BF16)
    dtb, _, _ = sb("dtb", [128, 4, 16], F32)
    dtab, _, _ = sb("dtab", [128, 4, 16], F32)
    exts = [sb("ext%d" % i, [128, 515], F32)[0] for i in range(3)]
    RC = cur[0]
    xatm, _, _ = sb("xatm", [128, 4, 512], BF16)
    xalast, _, _ = sb("xalast", [128, 512], F32)
    vnb, _, _ = sb("vnb", [128, 4, 512], BF16)
    ubuf, _, _ = sb("ubuf", [128, 4, 512], F32)
    zz, _, _ = sb("zz", [128, 4, 512], BF16)
    vn32, _, _ = sb("vn32", [128, 512], F32)
    st6, _, _ = sb("st6", [128, 8], F32)
    RC1 = cur[0]
    cur[0] = RC
    Eb, _, _ = sb("Eb", [128, 16, 128], BF16)
    Mb, _, _ = sb("Mb", [128, 16, 128], BF16)
    Mb2, _, _ = sb("Mb2", [128, 16, 128], BF16)
    ytm, _, _ = sb("ytm", [128, 1024], F32)
    ygb, _, _ = sb("ygb", [128, 8, 128], F32)
    sqb, _, _ = sb("sqb", [128, 8, 128], BF16)
    rsb, _, _ = sb("rsb", [128, 128], F32)
    assert cur[0] - RC >= 18 * 1024
    dtaU, _, _ = sb("dtaU", [128, 16, 128], BF16)
    cbm, _, _ = sb("cbm", [128, 4, 128], BF16)
    xdts = [sb("xdt%d" % i, [128, 1024], BF16)[0] for i in range(2)]
    xws = [sb("xw%d" % i, [128, 1024], BF16)[0] for i in range(2)]
    Btms = [sb("Btm%d" % i, [128, 512], BF16)[0] for i in range(2)]
    exbs = [sb("exb%d" % i, [128, 48], F32)[0] for i in range(2)]
    xDs = [sb("xD%d" % i, [128, 1024], BF16)[0] for i in range(2)]
    RC2 = cur[0]
    cur[0] = RC
    mrg, _, _ = sb("mrg", [128, 8, 512], BF16)
    msb, _, _ = sb("msb", [128, 8, 512], F32)
    sg, _, _ = sb("sg", [128, 3, 512], F32)
    RC3 = cur[0]
    cur[0] = max(RC1, RC2, RC3)
    TOT = cur[0]
    cur[0] = A0
    NSTG = 4
    stg32 = [sb("stg32_%d" % i, [128, 3072], F32)[0] for i in range(NSTG)]
    stgbf = [sb("stgbf_%d" % i, [128, 3072], BF16)[0] for i in range(NSTG)]
    sst, _, _ = sb("sst", [128, 6400], F32)
    assert cur[0] <= TOT, (cur[0], TOT)
    assert TOT <= 229376 - 64, TOT
    print('SBUF bytes/partition used', TOT)

    ps = nc.alloc_psum_tensor("ps", [128, 8, 512], F32)
    bank_rr = [0]

    nbank = [8]

    def bank(n=1):
        b = bank_rr[0] % nbank[0]
        if n == 2 and b % 2:
            b = (b + 1) % nbank[0]
        bank_rr[0] = (b + n) % nbank[0]
        return b

    def mm(out, lhsT, rhs, start=True, stop=True):
        P.op("pe", lambda e: e.matmul(out, lhsT=lhsT, rhs=rhs, start=start, stop=stop), [lhsT, rhs], [out])

    def tr(out, in_, ident):
        P.op("pe", lambda e: e.transpose(out=out, in_=in_, identity=ident), [in_, ident], [out])

    def act(out, in_, func, bias=None, scale=None):
        kw = {}
        reads = [in_]
        if bias is not None:
            kw["bias"] = bias
            if not isinstance(bias, float):
                reads.append(bias)
        if scale is not None:
            kw["scale"] = scale
            if not isinstance(scale, float):
                reads.append(scale)
        P.op("act", lambda e: e.activation(out=out, in_=in_, func=func, **kw), reads, [out])

    def tt(eng, out, in0, in1, op):
        P.op(eng, lambda e: e.tensor_tensor(out=out, in0=in0, in1=in1, op=op), [in0, in1], [out])

    def ts(eng, out, in0, s1, s2, op0, op1):
        reads = [in0] + [x for x in (s1, s2) if x is not None and not isinstance(x, (float, int))]
        P.op(eng, lambda e: e.tensor_scalar(out=out, in0=in0, scalar1=s1, scalar2=s2, op0=op0, op1=op1), reads, [out])

    def stt(eng, out, in0, scalar, in1, op0, op1):
        reads = [in0, in1] + ([] if isinstance(scalar, (float, int)) else [scalar])
        P.op(eng, lambda e: e.scalar_tensor_tensor(out=out, in0=in0, scalar=scalar, in1=in1, op0=op0, op1=op1), reads, [out])

    def rsq(out, in_, scale, eps, post=1.0):
        act(out, in_, AF.Ln, bias=float(eps), scale=float(scale))
        if post == 1.0:
            act(out, out, AF.Exp, scale=-0.5)
        else:
            act(out, out, AF.Exp, scale=-0.5, bias=float(np.log(post)))

    def cp(eng, out, in_):
        if eng == "act":
            act(out, in_, AF.Copy)
        else:
            P.op(eng, lambda e: e.tensor_copy(out=out, in_=in_), [in_], [out])

    def mset(eng, out, val):
        P.op(eng, lambda e: e.memset(out, val), [], [out])

    def dma(q, out, in_, rk=None, wk=None, slow=False):
        reads = [rk if rk is not None else in_]
        writes = [wk if wk is not None else out]
        if isinstance(rk, list):
            reads = rk
        if slow:
            P.dma(q, lambda e: e.dma_start(out=out, in_=in_, allow_slow_non_contiguous=True), reads, writes)
        else:
            P.dma(q, lambda e: e.dma_start(out=out, in_=in_), reads, writes)

    inst = [0]

    class _Vec:
        def __getitem__(self, key):
            rows, cols = key
            lo, hi = cols.start, cols.stop
            if hi <= 56:
                return vecG[inst[0] % 2][rows, lo:hi]
            assert lo >= 56
            return vecM[rows, lo - 56:hi - 56]
    vec = _Vec()

    def load_vec_g(l_next):
        dma("sp", vecG[(inst[0] + 1) % 2][:, :], vecs[l_next, :, 0:56], rk=("in", "vecs"))

    def load_vec_m(l_next):
        dma("sp", vecM[:, :], vecs[l_next, :, 56:NV], rk=("in", "vecs"))

    identf, U2f, U1f, onesf = cf[:, 0, :], cf[:, 1, :], cf[:, 2, :], cf[:, 3, :]
    identb, onesb, U1b = cb[:, 0, :], cb[:, 1, :], cb[:, 14, :]

    def band(kind, g):
        return cb[:, 2 + kind * 4 + g, :]

    st0 = sst
    dma("sp", st0[:, 0:17 * 128], cst.rearrange("p a b -> p (a b)"), rk=("in", "cst"))
    cp("dve", cf[:, :, :], st0[:, 0:4 * 128].rearrange("p (a b) -> p a b", b=128))
    cp("dve", cb[:, 0, :], st0[:, 0:128])
    cp("dve", cb[:, 1, :], st0[:, 3 * 128:4 * 128])
    cp("dve", cb[:, 2:14, :], st0[:, 5 * 128:17 * 128].rearrange("p (a b) -> p a b", b=128))
    cp("dve", cb[:, 14, :], st0[:, 2 * 128:3 * 128])
    LTf = st0[:, 4 * 128:5 * 128]
    pwv = sst[:, 2176:2176 + DEPTH * 512].rearrange("p (a b) -> p a b", b=128)
    dma("sp", pwv, pool_w.rearrange("l g c d -> c (l g) d"), rk=("in", "pool_w"))
    cp("act", plw[:, :, :], pwv)
    wsv = sst[:, 4224:4224 + DEPTH * 512].rearrange("p (a b) -> p a b", b=128)
    dma("sp", wsv, ws_in.rearrange("l h t s -> t (l h) s"), rk=("in", "ws"))
    tt("dve", wsv, wsv, LTf.unsqueeze(1).to_broadcast([128, DEPTH * 4, 128]), ALU.mult)
    for i in range(DEPTH * 4):
        b = bank()
        tr(ps[:, b, 0:128], wsv[:, i, :], identf)
        cp("act", wsmT[:, i, :], ps[:, b, 0:128])

    nst = {}
    jobn = [0]
    ceng = ["act", "dve"]

    gfold = sst[:, 6280:6280 + DEPTH * 8].rearrange("p (l k) -> p l k", k=8)
    dma("sp", gfold, vecs[:, :, 48:56].rearrange("l p k -> p l k"), rk=("in", "vecs"), slow=True)

    def precast(src, W, stores, name, scale=None):
        i = jobn[0] % NSTG
        jobn[0] += 1
        dma("sp", stg32[i][:, 0:W], src, rk=("in", name))
        if scale is not None:
            act(stgbf[i][:, 0:W], stg32[i][:, 0:W], AF.Copy, scale=scale)
        else:
            cp(ceng[jobn[0] % 2], stgbf[i][:, 0:W], stg32[i][:, 0:W])
        for (dst, sview) in stores:
            k = nst.get(name, 0)
            nst[name] = k + 1
            dma("act", dst, sview, wk=("scr", name, k), slow=True)

    def sv(i_, lo, n, inner):
        return stgbf[i_][:, lo:lo + n * inner].rearrange("p (a b) -> p a b", b=inner)

    for l in range(DEPTH):
        for f in range(2):
            nm = "gu%d_%d" % (f, l)
            for kc in range(8):
                for half in range(2):
                    i_ = jobn[0] % NSTG
                    precast(w_gu[f][l, kc * 128:(kc + 1) * 128, half * DFF:(half + 1) * DFF], DFF,
                            [(gu_s[f][l, :, :, kc, half * 128:(half + 1) * 128].rearrange("j p c -> p j c"),
                              sv(i_, 0, 22, 128))], nm)
            nm = "dn%d_%d" % (f, l)
            for kc in range(22):
                i_ = jobn[0] % NSTG
                precast(w_dn[f][l, kc * 128:(kc + 1) * 128, :], D,
                        [(dn_s[f][l, :, :, kc, :].rearrange("c p d -> p c d"), sv(i_, 0, 8, 128))], nm)
        for kc in range(8):
            rows = slice(kc * 128, (kc + 1) * 128)
            q, r = kc // 2, kc % 2
            i_ = jobn[0] % NSTG
            precast(w_in[l, rows, 0:512], 512, [(wtm_s[l, q, :, r, 0:512], stgbf[i_][:, 0:512])], "wtm_%d" % l)
            i_ = jobn[0] % NSTG
            precast(w_in[l, rows, 512:3584], 3072,
                    [(wfm_s[l, 0:12, :, kc, :].rearrange("q p c -> p q c"), sv(i_, 0, 12, 256))], "wfm_%d" % l)
            i_ = jobn[0] % NSTG
            precast(w_in[l, rows, 3584:3600], 16, [(wtm_s[l, q, :, r, 1024:1040], stgbf[i_][:, 0:16])], "wtm_%d" % l)
            i_ = jobn[0] % NSTG
            precast(w_in[l, rows, 3600:4112], 512, [(wfm_s[l, 12:14, :, kc, :].rearrange("q p c -> p q c"), sv(i_, 0, 2, 256))], "wfm_%d" % l)
            i_ = jobn[0] % NSTG
            precast(w_in[l, rows, 4112:4624], 512, [(wtm_s[l, q, :, r, 512:1024], stgbf[i_][:, 0:512])], "wtm_%d" % l)
            i_ = jobn[0] % NSTG
            precast(w_in[l, rows, 4624:7696], 3072,
                    [(wmg_s[l, :, :, 16 + b_ * 8 + kc, :].rearrange("c p d -> p c d"), sv(i_, b_ * 1024, 8, 128))
                     for b_ in range(3)], "wmg_%d" % l)
        for (wsrc, nk, blk0) in ((w_a, 4, 0), (w_b, 8, 4), (w_c, 4, 12)):
            for kc in range(nk):
                i_ = jobn[0] % NSTG
                precast(wsrc[l, kc * 128:(kc + 1) * 128, :], D,
                        [(wmg_s[l, :, :, blk0 + kc, :].rearrange("c p d -> p c d"), sv(i_, 0, 8, 128))], "wmg_%d" % l,
                        scale=(sst[:, 6280 + l * 8 + kc:6280 + l * 8 + kc + 1] if blk0 == 4 else None))
        for kc in range(8):
            i_ = jobn[0] % NSTG
            precast(w_o[l, kc * 128:(kc + 1) * 128, :], D,
                    [(wo_s[l, q_, :, :, kc, :].rearrange("p c d -> p c d"), sv(i_, q_ * 512, 4, 128)) for q_ in range(2)],
                    "wo_%d" % l)

    slot_rr = [0]

    def piece(src, n, name):
        s_ = slots[slot_rr[0] % NSLOT]
        slot_rr[0] += 1
        dma("sp", s_[:, 0:n], src, rk=[("scr", name, k) for k in range(nst[name])])
        return s_

    def rms_stats(T, src_chunks, sq_dst):
        b = bank()
        n = len(src_chunks)
        for i, c in enumerate(src_chunks):
            act(sq_dst(i), c, AF.Square)
            mm(ps[:, b, 0:T], onesb, sq_dst(i), start=(i == 0), stop=(i == n - 1))
        rsq(rstd[:, 0:T], ps[:, b, 0:T], 1.0 / D, EPS)

    def prenorm(T, gcol):
        rms_stats(T, [h[:, kc, 0:T] for kc in range(8)], lambda i: xn[:, i, 0:T])
        for kc in range(8):
            stt("dve", xn[:, kc, 0:T], h[:, kc, 0:T],
                vec[:, gcol + kc:gcol + kc + 1], rstd[:, 0:T], ALU.mult, ALU.mult)

    def ffn(l, f, T, hdst=None):
        hdst = h if hdst is None else hdst
        gpre = 0 if f == 0 else 32
        gpost = 8 if f == 0 else 40
        P.tag = 'ffn.pre'
        prenorm(T, gpre)
        P.tag = 'ffn.gu'
        for j in range(22):
            w = piece(gu_s[f][l, j].rearrange("p k c -> p (k c)"), 2048, "gu%d_%d" % (f, l))
            wv = w[:, 0:2048].rearrange("p (k c) -> p k c", c=256)
            bg, bu = bank(), bank()
            for kc in range(8):
                mm(ps[:, bg, 0:T], wv[:, kc, 0:128], xn[:, kc, 0:T], kc == 0, kc == 7)
            for kc in range(8):
                mm(ps[:, bu, 0:T], wv[:, kc, 128:256], xn[:, kc, 0:T], kc == 0, kc == 7)
            act(tA[:, 0:T], ps[:, bg, 0:T], AF.Silu)
            tt("dve", hid[:, j, 0:T], tA[:, 0:T], ps[:, bu, 0:T], ALU.mult)
        P.tag = 'ffn.dn'
        act(st6[:, 0:1], onesf[:, 0:1], AF.Ln)
        bst = bank()
        for c in range(9):
            if c < 8:
                w = piece(dn_s[f][l, c].rearrange("p k d -> p (k d)"), 22 * 128, "dn%d_%d" % (f, l))
                wv = w[:, 0:22 * 128].rearrange("p (k d) -> p k d", d=128)
                b = bank()
                if b == bst:
                    b = bank()
                for kc in range(22):
                    mm(ps[:, b, 0:T], wv[:, kc, :], hid[:, kc, 0:T], kc == 0, kc == 21)
                act(fsb[:, c, 0:T], ps[:, b, 0:T], AF.Copy, scale=vec[:, gpost + c:gpost + c + 1])
                act(xn[:, c, 0:T], ps[:, b, 0:T], AF.Square)
            if c > 0:
                mm(ps[:, bst, 0:T], onesb, xn[:, c - 1, 0:T], c == 1, c == 8)
        P.tag = 'ffn.post'
        rsq(rstd[:, 0:T], ps[:, bst, 0:T], 1.0 / D, EPS, post=0.5)
        for c in range(8):
            tq = (tA, tB, tC)[c % 3]
            tt("dve", tq[:, 0:T], fsb[:, c, 0:T], rstd[:, 0:T], ALU.mult)
            tt("pool" if c in (2, 5) else "dve", hdst[:, c, 0:T], tq[:, 0:T], h[:, c, 0:T], ALU.add)

    V_PS, V_CW, V_CB, V_DS, V_DTB, V_AL, V_GG, V_GB, V_BS, V_DB = 56, 60, 124, 140, 148, 164, 180, 692, 1204, 1716

    def mixer(l, T, CS, first, last, which, nxt):
        NCH = T // CS
        if nxt is not None:
            load_vec_g(nxt)
        P.tag = 'mx.pre'
        prenorm(T, 16)
        P.tag = 'mx.tm'
        act(tC[:, 0:16], vec[:, V_AL:V_AL + 16], AF.Exp)
        wq = []
        for q in range(4):
            w = piece(wtm_s[l, q].rearrange("p k c -> p (k c)"), 2 * 1040, "wtm_%d" % l)
            wq.append(w[:, 0:2080].rearrange("p (k c) -> p k c", c=1040))
        for c in range(NCH):
            tk = slice(c * CS, (c + 1) * CS)
            bxa, bv, bdt = bank(), bank(), bank()
            for (bb, c0, cn) in ((bxa, 0, 512), (bv, 512, 512), (bdt, 1024, 16)):
                for kc in range(8):
                    mm(ps[0:CS, bb, 0:cn], xn[:, kc, tk], wq[kc // 2][:, kc % 2, c0:c0 + cn], kc == 0, kc == 7)
            cp("act", xatm[0:CS, c, :], ps[0:CS, bxa, :])
            if c == NCH - 1:
                cp("act", xalast[0:CS, :], ps[0:CS, bxa, :])
            P.op("dve", (lambda bv_=bv: lambda e: e.bn_stats(out=st6[0:CS, 0:6], in_=ps[0:CS, bv_, :]))(),
                 [ps[0:CS, bv, :]], [st6[0:CS, 0:6]])
            P.op("dve", lambda e: e.bn_aggr(out=st6[0:CS, 6:8], in_=st6[0:CS, 0:6]), [st6[0:CS, 0:6]], [st6[0:CS, 6:8]])
            rsq(st6[0:CS, 7:8], st6[0:CS, 7:8], 1.0, EPS)
            ts("dve", st6[0:CS, 6:7], st6[0:CS, 6:7], st6[0:CS, 7:8], -1.0, ALU.mult, ALU.mult)
            act(vn32[0:CS, :], ps[0:CS, bv, :], AF.Identity, bias=st6[0:CS, 6:7], scale=st6[0:CS, 7:8])
            tt("dve", vn32[0:CS, :], vn32[0:CS, :], vec[0:CS, V_GG:V_GG + 512], ALU.mult)
            tt("dve", vn32[0:CS, :], vn32[0:CS, :], vec[0:CS, V_GB:V_GB + 512], ALU.add)
            cp("act", vnb[0:CS, c, :], vn32[0:CS, :])
            if which == 1:
                dma("act", o_gv[l], vn32[0:16, :], wk=("out", "gv", l))
            tt("dve", dtb[0:CS, c, :], ps[0:CS, bdt, 0:16], vec[0:CS, V_DTB:V_DTB + 16], ALU.add)
            act(dtb[0:CS, c, :], dtb[0:CS, c, :], AF.Exp)
            act(dtb[0:CS, c, :], dtb[0:CS, c, :], AF.Ln, bias=1.0)
            stt("dve", dtab[0:CS, c, :], dtb[0:CS, c, :], -1.0, tC[0:CS, 0:16], ALU.mult, ALU.mult)
        if last:
            if which == 0:
                dma("act", o_pool[0][l], xalast[113:128, :], wk=("out", "pool", l))
            else:
                dma("act", o_pool[1][l], xalast[1:16, :], wk=("out", "pool", l))
        P.tag = 'mx.pool'
        for c in range(NCH):
            b = bank()
            for g in range(4):
                gs = slice(g * 128, (g + 1) * 128)
                mm(ps[:, b, g * 128:g * 128 + CS], xatm[0:CS, c, gs], band(0 if (first and c == 0 and which == 0) else 1, g)[0:CS, 0:CS],
                   True, False)
                if c == 0:
                    mm(ps[:, b, g * 128:g * 128 + CS], xac[l][64:128, gs], band(2, g)[64:128, 0:CS], False, True)
                else:
                    mm(ps[:, b, g * 128:g * 128 + CS], xatm[64:128, c - 1, gs], band(2, g)[64:128, 0:CS], False, True)
            cp("act", zz[:, :, c * CS:(c + 1) * CS], ps[:, b, :].rearrange("p (g t) -> p g t", t=128)[:, :, 0:CS])
        cp("pool", xac[l][:, :], xatm[:, NCH - 1, :]) if CS == 128 else None
        Mbs = [Mb, Mb2]

        def S3(c):
            tk = slice(c * CS, (c + 1) * CS)
            xdt, Btm, xD = xdts[c % 2], Btms[c % 2], xDs[c % 2]
            b = bank()
            pb = ps[:, b, :].bitcast(BF16)
            for j in range(8):
                tr(pb[0:CS, j * 128:(j + 1) * 128], xs[:, j, tk], identb)
            b2 = bank()
            pb2 = ps[:, b2, :].bitcast(BF16)
            for g in range(4):
                tr(pb2[0:CS, g * 128:(g + 1) * 128], BC[:, g, tk], identb)
            dtv = dtb[0:CS, c, :].unsqueeze(2).to_broadcast([CS, 16, 64])
            p3 = pb[0:CS, 0:1024].rearrange("p (h q) -> p h q", q=64)
            tt("dve", xdt[0:CS, :].rearrange("p (h q) -> p h q", q=64), p3, dtv, ALU.mult)
            tt("dve", xD[0:CS, :].rearrange("p (h q) -> p h q", q=64), p3,
               vec[0:CS, V_DB:V_DB + 16].unsqueeze(2).to_broadcast([CS, 16, 64]), ALU.mult)
            cp("act", Btm[0:CS, :], pb2[0:CS, 0:512])

        def S1(c):
            tk = slice(c * CS, (c + 1) * CS)
            dta = dtab[0:CS, c, :]
            exb = exbs[c % 2]
            xdt, xw = xdts[c % 2], xws[c % 2]
            b = bank()
            mm(ps[0:CS, b, 0:16], U2f[0:CS, 0:CS], dta)
            mm(ps[0:CS, b, 16:32], U1f[0:CS, 0:CS], dta)
            mm(ps[:, b, 32:48], onesf[0:CS, :], dta)
            b2 = bank()
            for g in range(4):
                mm(ps[0:CS, b2, g * 128:g * 128 + CS], BC[:, g, tk], BC[:, 4 + g, tk])
            act(exb[0:CS, 0:32], ps[0:CS, b, 0:32], AF.Exp)
            act(exb[:, 32:48], ps[:, b, 32:48], AF.Exp)
            tt("dve", dtaU[0:CS, :, 0:CS], dta.unsqueeze(2).to_broadcast([CS, 16, CS]),
               U2f[0:CS, 0:CS].unsqueeze(1).to_broadcast([CS, 16, CS]), ALU.mult)
            tt("dve", cbm[0:CS, :, 0:CS], ps[0:CS, b2, :].rearrange("p (g t) -> p g t", t=128)[:, :, 0:CS],
               U2f[0:CS, 0:CS].unsqueeze(1).to_broadcast([CS, 4, CS]), ALU.mult)
            tt("pool", xw[0:CS, :].rearrange("p (h q) -> p h q", q=64), xdt[0:CS, :].rearrange("p (h q) -> p h q", q=64),
               exb[0:CS, 16:32].unsqueeze(2).to_broadcast([CS, 16, 64]), ALU.mult)

        def S2(c):
            Mc = Mbs[c % 2]
            for hf in range(2):
                b = bank(2)
                for q4 in range(2):
                    hq = hf * 2 + q4
                    if CS == 128:
                        mm(ps[0:CS, b + q4, :], U1b[0:CS, 0:CS], dtaU[0:CS, hq * 4:(hq + 1) * 4, :].rearrange("p a b -> p (a b)"))
                    else:
                        for hh in range(4):
                            mm(ps[0:CS, b + q4, hh * 128:hh * 128 + CS], U1b[0:CS, 0:CS], dtaU[0:CS, hq * 4 + hh, 0:CS])
                act(Eb[0:CS, hf * 8:(hf + 1) * 8, 0:CS],
                    ps[0:CS, b:b + 2, :].rearrange("p a (h t) -> p (a h) t", t=128)[:, :, 0:CS], AF.Exp)
            tt("dve", Mc[0:CS, :, 0:CS].rearrange("p (g r) t -> p g r t", r=4),
               Eb[0:CS, :, 0:CS].rearrange("p (g r) t -> p g r t", r=4),
               cbm[0:CS, :, 0:CS].unsqueeze(2).to_broadcast([CS, 4, 4, CS]), ALU.mult)

        stb = {}

        def S4(c):
            tk = slice(c * CS, (c + 1) * CS)
            exb = exbs[c % 2]
            Mc, xdt, xw, Btm, xD = Mbs[c % 2], xdts[c % 2], xws[c % 2], Btms[c % 2], xDs[c % 2]
            ecum = exb[0:CS, 0:16]
            by = bank(2)
            for hb in range(2):
                mm(ps[0:CS, by + hb, :], identb[0:CS, 0:CS], xD[0:CS, hb * 512:(hb + 1) * 512], True, False)
            for hh in range(16):
                mm(ps[0:CS, by + hh // 8, (hh % 8) * 64:(hh % 8) * 64 + 64], Mc[0:CS, hh, 0:CS], xdt[0:CS, hh * 64:(hh + 1) * 64],
                   False, hh % 8 == 7)
            bo = bank(2)
            for g in range(4):
                mm(ps[0:CS, bo + g // 2, (g % 2) * 256:(g % 2) * 256 + 256], BC[:, 4 + g, tk], Sbf1[:, g * 256:(g + 1) * 256])
            bs_ = 6
            for g in range(4):
                mm(ps[:, bs_ + g // 2, (g % 2) * 256:(g % 2) * 256 + 256], Btm[0:CS, g * 128:(g + 1) * 128],
                   xw[0:CS, g * 256:(g + 1) * 256])
            stb[c] = bs_
            pyo = ps[0:CS, bo:bo + 2, :].rearrange("p a (h q) -> p (a h) q", q=64)
            pyd = ps[0:CS, by:by + 2, :].rearrange("p a (h q) -> p (a h) q", q=64)
            y3 = ytm[0:CS, :].rearrange("p (h q) -> p h q", q=64)
            tt("dve", y3, pyo, ecum.unsqueeze(2).to_broadcast([CS, 16, 64]), ALU.mult)
            tt("dve", y3, y3, pyd, ALU.add)

        def S5(c):
            tk = slice(c * CS, (c + 1) * CS)
            byt = bank(2)
            for j in range(8):
                tr(ps[:, byt + j // 4, (j % 4) * 128:(j % 4) * 128 + CS], ytm[0:CS, j * 128:(j + 1) * 128], identf[0:CS, 0:CS])
            pyt = ps[:, byt:byt + 2, :].rearrange("p a (j t) -> p (a j) t", t=128)[:, :, 0:CS]
            tt("dve", ygb[:, :, 0:CS], pyt, sz[:, :, tk], ALU.mult)
            act(sqb[:, :, 0:CS], ygb[:, :, 0:CS], AF.Square)

        def S6(c):
            tk = slice(c * CS, (c + 1) * CS)
            elast = exbs[c % 2][:, 32:48]
            b = bank()
            for j in range(8):
                mm(ps[:, b, 0:CS], onesb, sqb[:, j, 0:CS], j == 0, j == 7)
            rsq(rsb[:, 0:CS], ps[:, b, 0:CS], 1.0 / D, EPS)
            tt("dve", yb[:, :, tk], ygb[:, :, 0:CS], rsb[:, 0:CS].unsqueeze(1).to_broadcast([128, 8, CS]), ALU.mult)
            bs_ = stb[c]
            s3 = S32[l][:, :].rearrange("p (h q) -> p h q", q=64)
            tt("pool", s3, s3, elast.unsqueeze(2).to_broadcast([128, 16, 64]), ALU.mult)
            tt("dve", S32[l][:, :].rearrange("p (a n) -> p a n", a=2), S32[l][:, :].rearrange("p (a n) -> p a n", a=2),
               ps[:, bs_:bs_ + 2, :], ALU.add)
            cp("act", Sbf1[:, :], S32[l][:, :])

        ssd_pre = [None]

        def _ssd_pre():
            cp("act", Sbf1[:, :], S32[l][:, :])
            nbank[0] = 6
            S3(0)
            S1(0)
        ssd_pre[0] = _ssd_pre
        P.tag = 'mx.fm'
        accs = [tA, tB, tC]
        pend = []
        for pi in (12, 13, 0, 1, 2, 3, 4, 5, 6, 7, 8, 9, 10, 11):
            w = piece(wfm_s[l, pi].rearrange("p k c -> p (k c)"), 2048, "wfm_%d" % l)
            wv = w[:, 0:2048].rearrange("p (k c) -> p k c", c=256)
            for ci in range(2):
                b = bank()
                for kc in range(8):
                    mm(ps[:, b, 0:T], wv[:, kc, ci * 128:(ci + 1) * 128], xn[:, kc, 0:T], kc == 0, kc == 7)
                if pi >= 12:
                    cp("act", ubuf[:, (pi - 12) * 2 + ci, 0:T], ps[:, b, 0:T])
                elif pi < 4:
                    act(sz[:, pi * 2 + ci, 0:T], ps[:, b, 0:T], AF.Silu)
                else:
                    j = (pi - 4) * 2 + ci
                    ext = exts[j % 3]
                    acc = accs[j % 3]
                    cp("pool", ext[:, 0:3], hist[l][:, j, :])
                    cp("act", ext[:, 3:3 + T], ps[:, b, 0:T])
                    cp("pool", hist[l][:, j, :], ext[:, T:T + 3])
                    cw = V_CW + j * 4
                    act(acc[:, 0:T], ext[:, 0:T], AF.Identity, bias=vec[:, V_CB + j:V_CB + j + 1], scale=vec[:, cw:cw + 1])
                    stt("dve", acc[:, 0:T], ext[:, 1:1 + T], vec[:, cw + 1:cw + 2], acc[:, 0:T], ALU.mult, ALU.add)
                    stt("dve", acc[:, 0:T], ext[:, 2:2 + T], vec[:, cw + 2:cw + 3], acc[:, 0:T], ALU.mult, ALU.add)
                    stt("dve", acc[:, 0:T], ext[:, 3:3 + T], vec[:, cw + 3:cw + 4], acc[:, 0:T], ALU.mult, ALU.add)
                    dst = xs[:, j, 0:T] if j < 8 else BC[:, j - 8, 0:T]
                    for (d_, a_) in pend:
                        act(d_, a_, AF.Silu)
                    pend = [(dst, acc[:, 0:T])]
        for (d_, a_) in pend:
            act(d_, a_, AF.Silu)
        for g in range(4):
            b = bank()
            mm(ps[:, b, 0:T], plw[:, l * 4 + g, :], zz[:, g, 0:T])
            act(ya[:, g, 0:T], ps[:, b, 0:T], AF.Copy, scale=vec[:, V_PS + g:V_PS + g + 1])
        ssd_pre[0]()
        for c in range(NCH):
            b = bank()
            for hd in range(4):
                mm(ps[:, b, hd * 128:hd * 128 + CS], vnb[0:CS, c, hd * 128:(hd + 1) * 128],
                   wsmT[0:CS, l * 4 + hd, 0:CS])
            pv = ps[:, b, :].rearrange("p (g t) -> p g t", t=128)[:, :, 0:CS]
            tv = tA[:, 0:512].rearrange("p (g t) -> p g t", t=128)[:, :, 0:CS]
            tt("dve", tv, pv, vec[:, V_BS:V_BS + 512].rearrange("p (g t) -> p g t", t=128)[:, :, 0:CS], ALU.add)
            tt("dve", yc[:, :, c * CS:(c + 1) * CS], tv, ubuf[:, :, c * CS:(c + 1) * CS], ALU.mult)
        P.tag = 'mx.ssd'
        S2(0)
        for c in range(NCH):
            S4(c)
            if c + 1 < NCH:
                S3(c + 1)
                S1(c + 1)
            S5(c)
            if c + 1 < NCH:
                S2(c + 1)
            S6(c)
        nbank[0] = 8
        if CS < 128:
            pass
        if last:
            dma("act", o_conv[which][l], hist[l][:, :, :], wk=("out", "conv", l))
            dma("act", o_ssm[which][l], S32[l][:, :], wk=("out", "ssm", l))
        P.tag = 'mx.merge'
        bst = bank()
        for c in range(8):
            w0 = piece(wmg_s[l, c, :, 0:20, :].rearrange("p k d -> p (k d)"), 2560, "wmg_%d" % l)
            w1 = piece(wmg_s[l, c, :, 20:40, :].rearrange("p k d -> p (k d)"), 2560, "wmg_%d" % l)
            wv0 = w0[:, 0:2560].rearrange("p (k d) -> p k d", d=128)
            wv1 = w1[:, 0:2560].rearrange("p (k d) -> p k d", d=128)

            def wvk(k_):
                return wv0[:, k_, :] if k_ < 20 else wv1[:, k_ - 20, :]
            banks = []
            for _ in range(6):
                b = bank()
                if b == bst:
                    b = bank()
                banks.append(b)
            for kc in range(4):
                mm(ps[:, banks[0], 0:T], wvk(kc), ya[:, kc, 0:T], kc == 0, kc == 3)
            for kc in range(8):
                mm(ps[:, banks[1], 0:T], wvk(4 + kc), yb[:, kc, 0:T], kc == 0, kc == 7)
            for kc in range(4):
                mm(ps[:, banks[2], 0:T], wvk(12 + kc), yc[:, kc, 0:T], kc == 0, kc == 3)
            for b_ in range(3):
                for kc in range(8):
                    mm(ps[:, banks[3 + b_], 0:T], wvk(16 + b_ * 8 + kc), xn[:, kc, 0:T], kc == 0, kc == 7)
                act(sg[:, b_, 0:T], ps[:, banks[3 + b_], 0:T], AF.Sigmoid)
            tt("dve", sg[:, 0, 0:T], sg[:, 0, 0:T], ps[:, banks[0], 0:T], ALU.mult)
            tt("dve", sg[:, 1, 0:T], sg[:, 1, 0:T], ps[:, banks[1], 0:T], ALU.mult)
            tt("dve", sg[:, 2, 0:T], sg[:, 2, 0:T], ps[:, banks[2], 0:T], ALU.mult)
            tt("pool", sg[:, 0, 0:T], sg[:, 0, 0:T], sg[:, 1, 0:T], ALU.add)
            tt("pool", mrg[:, c, 0:T], sg[:, 0, 0:T], sg[:, 2, 0:T], ALU.add)
        P.tag = 'mx.out'
        if nxt is not None:
            load_vec_m(nxt)
        wos = []
        for q in range(4):
            w = piece(wo_s[l, q // 2, :, (q % 2) * 2:(q % 2) * 2 + 2].rearrange("p c k d -> p (c k d)"), 2048, "wo_%d" % l)
            wos.append(w[:, 0:2048].rearrange("p (c k d) -> p c k d", k=8, d=128))
        act(st6[:, 0:1], onesf[:, 0:1], AF.Ln)
        bst = bank()
        for c in range(9):
            if c < 8:
                b = bank()
                if b == bst:
                    b = bank()
                for kc in range(8):
                    mm(ps[:, b, 0:T], wos[c // 2][:, c % 2, kc, :], mrg[:, kc, 0:T], kc == 0, kc == 7)
                act(msb[:, c, 0:T], ps[:, b, 0:T], AF.Copy, scale=vec[:, 24 + c:24 + c + 1])
                act(xn[:, c, 0:T], ps[:, b, 0:T], AF.Square)
            if c > 0:
                mm(ps[:, bst, 0:T], onesb, xn[:, c - 1, 0:T], c == 1, c == 8)
        rsq(rstd[:, 0:T], ps[:, bst, 0:T], 1.0 / D, EPS)
        for c in range(8):
            tq = (tA, tB, tC)[c % 3]
            tt("dve", tq[:, 0:T], msb[:, c, 0:T], rstd[:, 0:T], ALU.mult)
            tt("dve", h[:, c, 0:T], tq[:, 0:T], h[:, c, 0:T], ALU.add)

    def run_tile(T, CS, first, last, which, src, dst, final):
        dma("sp", h[:, :, 0:T], src, rk=("in", "x"))
        for l in range(DEPTH):
            if inst[0] == 0:
                dma("sp", vecG[0][:, :], vecs[0, :, 0:56], rk=("in", "vecs"))
                load_vec_m(0)
            nxt = None if (final and l == DEPTH - 1) else (l + 1) % DEPTH
            if first:
                if which == 0:
                    mset("dve", S32[l][:, :], 0.0)
                    mset("dve", hist[l][:, :, :], 0.0)
                    mset("dve", xac[l][:, :], 0.0)
                else:
                    dma("sp", S32[l][:, :], ssm_st[l], rk=("in", "ssm"))
                    dma("sp", hist[l][:, :, :], conv_st[l], rk=("in", "conv"))
                    mset("dve", vn32[:, :], 0.0)
                    dma("sp", vn32[113:128, :], pool_st[l], rk=("in", "poolst"))
                    cp("act", xac[l][:, :], vn32[:, :])
            ffn(l, 0, T)
            mixer(l, T, CS, first, last, which, nxt)
            ffn(l, 1, T, hdst=(msb if l == DEPTH - 1 else None))
            inst[0] += 1
        dma("act", dst, msb[:, :, 0:T], wk=("out", "y", which, id(dst)))

    for t in range(NT):
        run_tile(512, 128, t == 0, t == NT - 1, 0, xT[t], yT[t], (not has_sample) and t == NT - 1)
    if has_sample:
        run_tile(16, 16, True, True, 1, xsT, ysT, True)

    eng_obj = {"pe": nc.tensor, "act": nc.scalar, "dve": nc.vector, "pool": nc.gpsimd, "sp": nc.sync}
    names = ["pe", "act", "dve", "pool"] + ["d%d" % i for i in range(NDS)]
    import contextlib
    with contextlib.ExitStack() as es_:
        sems = {n: es_.enter_context(nc.semaphore("s_" + n)) for n in names}
        block = es_.enter_context(nc.Block())
        fin = [("d%d" % i, P.dma_tot[i]) for i in range(NDS) if P.dma_tot[i] > 0]

        def replay(name, e, tail=False):
            for waits, fn, sk, inc in P.q[name]:
                for (wsk, v) in waits:
                    e.wait_ge(sems[wsk], v)
                fn(e).then_inc(sems[sk], inc)
            if tail:
                for (wsk, v) in fin:
                    e.wait_ge(sems[wsk], v)

        @block.sync
        def _(e):
            replay("sp", e)

        @block.tensor
        def _(e):
            replay("pe", e)

        @block.scalar
        def _(e):
            replay("act", e, tail=True)

        @block.vector
        def _(e):
            replay("dve", e)

        @block.gpsimd
        def _(e):
            replay("pool", e)
    return nc, {e: len(P.q[e]) for e in P.q}, P.tags


def make_consts():
    k = np.arange(128)[:, None]
    t = np.arange(128)[None, :]
    c = np.zeros((128, 17, 128), np.float32)
    c[:, 0] = (k == t)
    c[:, 1] = (k <= t)
    c[:, 2] = (k > t)
    c[:, 3] = 1.0
    c[:, 4] = (k >= t)
    for g, w in enumerate((2, 4, 8, 16)):
        inwin = (k <= t) & (k > t - w)
        cnt0 = np.minimum(t + 1, w).astype(np.float32)
        c[:, 5 + g] = inwin / cnt0 - (k == t)
        c[:, 9 + g] = inwin / np.float32(w) - (k == t)
        c[:, 13 + g] = ((t + 128 - k) <= (w - 1)) / np.float32(w)
    return c


def pack_vecs(inp, depth):
    v = np.zeros((depth, 128, NV), np.float32)

    def fm(a, n):
        return np.transpose(a.reshape(depth, n, 128), (0, 2, 1))
    o = 0
    for nm in ("ffn1_pre_g", "ffn1_post_g", "mix_pre_g", "mix_post_g", "ffn2_pre_g", "ffn2_post_g", "ssm_norm_g"):
        v[:, :, o:o + 8] = fm(inp[nm], 8)
        o += 8
    v[:, :, 56:60] = fm(inp["pool_scale"], 4)
    cw = inp["ssm_conv_w"].reshape(depth, 4, 16, 128)
    v[:, :, 60:124] = np.transpose(cw, (0, 3, 2, 1)).reshape(depth, 128, 64)
    v[:, :, 124:140] = fm(inp["ssm_conv_b"], 16)
    ds = inp["ssm_d"].reshape(depth, 8, 2)
    v[:, :, 140:148] = np.repeat(np.transpose(ds, (0, 2, 1)), 64, axis=1)
    v[:, :, 148:164] = np.broadcast_to(inp["ssm_dt_bias"][:, None, :], (depth, 128, 16))
    v[:, :, 164:180] = np.broadcast_to(inp["ssm_a_log"][:, None, :], (depth, 128, 16))
    v[:, :, 180:692] = np.broadcast_to(inp["gmlp_norm_g"][:, None, :], (depth, 128, 512))
    v[:, :, 692:1204] = np.broadcast_to(inp["gmlp_norm_b"][:, None, :], (depth, 128, 512))
    v[:, :, 1204:1716] = np.broadcast_to(inp["gmlp_bs"].reshape(depth, 1, 512), (depth, 128, 512))
    v[:, :, 1716:1732] = np.broadcast_to(inp["ssm_d"][:, None, :], (depth, 128, 16))
    return v


_CACHE = {}


def run(inp, n_cores, seq, depth, runner=None):
    NT = seq // 512
    key = (NT, depth)
    if key not in _CACHE:
        _CACHE[key] = build(NT, depth)
    nc = _CACHE[key][0]
    f32 = np.float32
    vecs = pack_vecs(inp, depth)
    cst = make_consts()
    shared = {
        "w_gu1": inp["ffn1_w_gu"], "w_gu2": inp["ffn2_w_gu"], "w_dn1": inp["ffn1_w_down"], "w_dn2": inp["ffn2_w_down"],
        "w_in": inp["w_in"], "w_a": inp["w_branch_a"], "w_b": inp["w_branch_b"], "w_c": inp["w_branch_c"],
        "w_o": inp["w_out"], "vecs": vecs, "pool_w": inp["pool_w"], "ws": inp["gmlp_ws"], "cst": cst,
    }
    shared = {k: np.ascontiguousarray(v, dtype=f32) for k, v in shared.items()}
    in_maps = []
    for c in range(n_cores):
        xp = inp["x_prompt"][c].reshape(NT, 512, 8, 128)
        m = dict(shared)
        m["xT"] = np.ascontiguousarray(np.transpose(xp, (0, 3, 2, 1)), dtype=f32)
        m["xsT"] = np.ascontiguousarray(np.transpose(inp["x_sample"][c].reshape(16, 8, 128), (2, 1, 0)), dtype=f32)
        m["pool_st"] = np.ascontiguousarray(inp["state_pool"][:, c], dtype=f32)
        m["conv_st"] = np.ascontiguousarray(np.transpose(inp["state_conv"][:, c].reshape(depth, 3, 16, 128), (0, 3, 2, 1)), dtype=f32)
        m["ssm_st"] = np.ascontiguousarray(np.transpose(inp["state_ssm"][:, c].reshape(depth, 1024, 128), (0, 2, 1)), dtype=f32)
        in_maps.append(m)
    if runner is None:
        res = run_bass_kernel_spmd(nc, in_maps, core_ids=list(range(n_cores))).results
    else:
        res = runner(nc, in_maps)
    B = n_cores
    y_p = np.stack([np.transpose(r["yT"], (0, 3, 2, 1)).reshape(seq, D) for r in res])
    y_s = np.stack([np.transpose(r["ysT"], (2, 1, 0)).reshape(16, D) for r in res])

    def conv_back(a):
        return np.transpose(a, (0, 3, 2, 1)).reshape(depth, 3, 2048)

    def ssm_back(a):
        return np.transpose(a, (0, 2, 1)).reshape(depth, 16, 64, 128)
    outs = [y_p, y_s]
    for sfx in ("p", "s"):
        outs.append(np.stack([r["npool_" + sfx] for r in res], axis=1))
        outs.append(np.stack([conv_back(r["nconv_" + sfx]) for r in res], axis=1))
        outs.append(np.stack([ssm_back(r["nssm_" + sfx]) for r in res], axis=1))
    outs.append(np.stack([r["gv_s"] for r in res], axis=1))
    return tuple(np.ascontiguousarray(o, dtype=f32) for o in outs)


def kernel(**inputs):
    inp = {k: np.asarray(v) for k, v in inputs.items()}
    return run(inp, 8, 8192, 4)
```

```python
import numpy as np
import concourse.bass as bass
import concourse.mybir as mybir
from concourse.bass_utils import run_bass_kernel_spmd

F32, BF16 = mybir.dt.float32, mybir.dt.bfloat16
AF = mybir.ActivationFunctionType
ALU = mybir.AluOpType

D = 1024
DFF = 2816
INC = 7696
EPS = 1e-6
GR = 256
NDS = 8
NV = 1732


def _es(dt):
    return 2 if dt == BF16 else 4


def rng(ap):
    t = ap.tensor
    es = _es(ap.dtype)
    shp = list(t.shape)
    rowb = int(np.prod(shp[1:])) * _es(t.dtype)
    f0 = (ap.offset * es) % rowb
    ext = 1
    for st, c in list(ap.ap)[1:]:
        ext += (c - 1) * abs(st)
    if type(t).__name__.startswith("PSum"):
        return "ps", f0, f0 + ext * es, 2048
    base = t.manual_sbuf_range[0]
    return "sb", base + f0, base + f0 + ext * es, GR


class Prog:
    ENG = ["pe", "act", "dve", "pool", "sp"]

    def __init__(s):
        s.q = {e: [] for e in s.ENG}
        s.cnt = {e: 0 for e in s.ENG}
        s.waited = {e: {} for e in s.ENG}
        s.lw = {}
        s.rd = {}
        s.dma_tot = [0] * NDS
        s.dma_rr = 0
        s.tag = ''
        s.tags = {e: [] for e in s.ENG}

    def keys(s, x):
        if isinstance(x, tuple):
            return [x]
        sp, lo, hi, g = rng(x)
        return [(sp, i) for i in range(lo // g, (hi - 1) // g + 1)]

    def _deps(s, eng, reads, writes):
        need = {}

        def add(sk, v):
            if need.get(sk, 0) < v:
                need[sk] = v
        for r in reads:
            for k in s.keys(r):
                if k in s.lw:
                    add(*s.lw[k])
        for w in writes:
            for k in s.keys(w):
                if k in s.lw:
                    add(*s.lw[k])
                for sk, v in s.rd.get(k, {}).items():
                    add(sk, v)
        waits = []
        for sk, v in need.items():
            if sk == eng and eng == "pe":
                continue
            if s.waited[eng].get(sk, 0) >= v:
                continue
            s.waited[eng][sk] = v
            waits.append((sk, v))
        return waits

    def _commit(s, ev, reads, writes):
        for r in reads:
            for k in s.keys(r):
                d = s.rd.setdefault(k, {})
                if d.get(ev[0], 0) < ev[1]:
                    d[ev[0]] = ev[1]
        for w in writes:
            for k in s.keys(w):
                s.lw[k] = ev
                s.rd[k] = {}

    def op(s, eng, fn, reads, writes):
        waits = s._deps(eng, reads, writes)
        s.cnt[eng] += 1
        s.q[eng].append((waits, fn, eng, 1))
        s.tags[eng].append(s.tag)
        s._commit((eng, s.cnt[eng]), reads, writes)

    def dma(s, qeng, fn, reads, writes):
        waits = s._deps(qeng, reads, writes)
        i = s.dma_rr
        s.dma_rr = (i + 1) % NDS
        sk = "d%d" % i
        if s.dma_tot[i] > 0 and s.waited[qeng].get(sk, 0) < s.dma_tot[i]:
            waits.append((sk, s.dma_tot[i]))
            s.waited[qeng][sk] = s.dma_tot[i]
        s.dma_tot[i] += 16
        s.q[qeng].append((waits, fn, sk, 16))
        s.tags[qeng].append(s.tag)
        s._commit((sk, s.dma_tot[i]), reads, writes)


def build(NT, DEPTH, has_sample=True):
    nc = bass.Bass("TRN2", target_bir_lowering=False)
    P = Prog()

    def din(name, shape, dt=F32):
        return nc.dram_tensor(name, list(shape), dt, kind="ExternalInput").ap()

    def dout(name, shape):
        return nc.dram_tensor(name, list(shape), F32, kind="ExternalOutput").ap()

    def dscr(name, shape):
        return nc.dram_tensor(name, list(shape), BF16, kind="Internal").ap()

    xT = din("xT", [NT, 128, 8, 512])
    xsT = din("xsT", [128, 8, 16])
    pool_st = din("pool_st", [DEPTH, 15, 512])
    conv_st = din("conv_st", [DEPTH, 128, 16, 3])
    ssm_st = din("ssm_st", [DEPTH, 128, 1024])
    w_gu = [din("w_gu1", [DEPTH, D, 2 * DFF]), din("w_gu2", [DEPTH, D, 2 * DFF])]
    w_dn = [din("w_dn1", [DEPTH, DFF, D]), din("w_dn2", [DEPTH, DFF, D])]
    w_in = din("w_in", [DEPTH, D, INC])
    w_a = din("w_a", [DEPTH, 512, D])
    w_b = din("w_b", [DEPTH, D, D])
    w_c = din("w_c", [DEPTH, 512, D])
    w_o = din("w_o", [DEPTH, D, D])
    vecs = din("vecs", [DEPTH, 128, NV])
    pool_w = din("pool_w", [DEPTH, 4, 128, 128])
    ws_in = din("ws", [DEPTH, 4, 128, 128])
    cst = din("cst", [128, 17, 128])

    yT = dout("yT", [NT, 128, 8, 512])
    ysT = dout("ysT", [128, 8, 16])
    o_pool = [dout("npool_p", [DEPTH, 15, 512]), dout("npool_s", [DEPTH, 15, 512])]
    o_conv = [dout("nconv_p", [DEPTH, 128, 16, 3]), dout("nconv_s", [DEPTH, 128, 16, 3])]
    o_ssm = [dout("nssm_p", [DEPTH, 128, 1024]), dout("nssm_s", [DEPTH, 128, 1024])]
    o_gv = dout("gv_s", [DEPTH, 16, 512])

    gu_s = [dscr("gu1_s", [DEPTH, 22, 128, 8, 256]), dscr("gu2_s", [DEPTH, 22, 128, 8, 256])]
    dn_s = [dscr("dn1_s", [DEPTH, 8, 128, 22, 128]), dscr("dn2_s", [DEPTH, 8, 128, 22, 128])]
    wtm_s = dscr("wtm_s", [DEPTH, 4, 128, 2, 1040])
    wfm_s = dscr("wfm_s", [DEPTH, 14, 128, 8, 256])
    wmg_s = dscr("wmg_s", [DEPTH, 8, 128, 40, 128])
    wo_s = dscr("wo_s", [DEPTH, 2, 128, 4, 8, 128])

    SB0 = 20480
    cur = [SB0]

    def sb(name, shape, dt, at=None):
        nb = int(np.prod(shape[1:])) * _es(dt)
        nb = (nb + GR - 1) // GR * GR
        if at is None:
            off = cur[0]
            cur[0] += nb
        else:
            off = at
        return nc.alloc_sbuf_tensor_at(name, list(shape), dt, offset=off), off, nb

    h, _, _ = sb("h", [128, 8, 512], F32)
    S32 = [sb("S32_%d" % l, [128, 1024], F32)[0] for l in range(DEPTH)]
    Sbf1, _, _ = sb("Sbf1", [128, 1024], BF16)
    hist = [sb("hist_%d" % l, [128, 16, 3], F32)[0] for l in range(DEPTH)]
    xac = [sb("xac_%d" % l, [128, 512], BF16)[0] for l in range(DEPTH)]
    cf, _, _ = sb("cf", [128, 4, 128], F32)
    cb, _, _ = sb("cb", [128, 15, 128], BF16)
    wsmT, _, _ = sb("wsmT", [128, DEPTH * 4, 128], BF16)
    plw, _, _ = sb("plw", [128, DEPTH * 4, 128], BF16)
    vecG = [sb("vecG%d" % i, [128, 56], F32)[0] for i in range(2)]
    vecM, _, _ = sb("vecM", [128, NV - 56], F32)
    NSLOT = 6
    SLOTN = 2816
    slots = [sb("slot%d" % i, [128, SLOTN], BF16)[0] for i in range(NSLOT)]
    A0 = cur[0]
    xn, _, _ = sb("xn", [128, 8, 512], BF16)
    rstd, _, _ = sb("rstd", [128, 512], F32)
    tA, _, _ = sb("tA", [128, 512], F32)
    tB, _, _ = sb("tB", [128, 512], F32)
    tC, _, _ = sb("tC", [128, 512], F32)
    RA = cur[0]
    hid, _, _ = sb("hid", [128, 22, 512], BF16)
    fsb, _, _ = sb("fsb", [128, 8, 512], F32)
    RAend = cur[0]
    cur[0] = RA
    sz, _, _ = sb("sz", [128, 8, 512], BF16)
    xs, _, _ = sb("xs", [128, 8, 512], BF16)
    BC, _, _ = sb("BC", [128, 8, 512], BF16)
    yb, _, _ = sb("yb", [128, 8, 512], BF16)
    ya, _, _ = sb("ya", [128, 4, 512], BF16)
    assert cur[0] <= RAend
    cur[0] = RAend
    yc, _, _ = sb("yc", [128, 4, 512], BF16)
    dtb, _, _ = sb("dtb", [128, 4, 16], F32)
    dtab, _, _ = sb("dtab", [128, 4, 16], F32)
    exts = [sb("ext%d" % i, [128, 515], F32)[0] for i in range(3)]
    RC = cur[0]
    xatm, _, _ = sb("xatm", [128, 4, 512], BF16)
    xalast, _, _ = sb("xalast", [128, 512], F32)
    vnb, _, _ = sb("vnb", [128, 4, 512], BF16)
    ubuf, _, _ = sb("ubuf", [128, 4, 512], F32)
    zz, _, _ = sb("zz", [128, 4, 512], BF16)
    vn32, _, _ = sb("vn32", [128, 512], F32)
    st6, _, _ = sb("st6", [128, 8], F32)
    RC1 = cur[0]
    cur[0] = RC
    Eb, _, _ = sb("Eb", [128, 16, 128], BF16)
    Mb, _, _ = sb("Mb", [128, 16, 128], BF16)
    Mb2, _, _ = sb("Mb2", [128, 16, 128], BF16)
    ytm, _, _ = sb("ytm", [128, 1024], F32)
    ygb, _, _ = sb("ygb", [128, 8, 128], F32)
    sqb, _, _ = sb("sqb", [128, 8, 128], BF16)
    rsb, _, _ = sb("rsb", [128, 128], F32)
    assert cur[0] - RC >= 18 * 1024
    dtaU, _, _ = sb("dtaU", [128, 16, 128], BF16)
    cbm, _, _ = sb("cbm", [128, 4, 128], BF16)
    xdts = [sb("xdt%d" % i, [128, 1024], BF16)[0] for i in range(2)]
    xws = [sb("xw%d" % i, [128, 1024], BF16)[0] for i in range(2)]
    Btms = [sb("Btm%d" % i, [128, 512], BF16)[0] for i in range(2)]
    exbs = [sb("exb%d" % i, [128, 48], F32)[0] for i in range(2)]
    xDs = [sb("xD%d" % i, [128, 1024], BF16)[0] for i in range(2)]
    RC2 = cur[0]
    cur[0] = RC
    mrg, _, _ = sb("mrg", [128, 8, 512], BF16)
    msb, _, _ = sb("msb", [128, 8, 512], F32)
    sg, _, _ = sb("sg", [128, 3, 512], F32)
    RC3 = cur[0]
    cur[0] = max(RC1, RC2, RC3)
    TOT = cur[0]
    assert TOT - (RC + 24 * 1024) >= 16 * 1024
    xpre = nc.alloc_sbuf_tensor_at("xpre", [128, 8, 512], F32, offset=RC + 24 * 1024)
    cur[0] = A0
    NSTG = 4
    stg32 = [sb("stg32_%d" % i, [128, 3072], F32)[0] for i in range(NSTG)]
    stgbf = [sb("stgbf_%d" % i, [128, 3072], BF16)[0] for i in range(NSTG)]
    sst, _, _ = sb("sst", [128, 6400], F32)
    assert cur[0] <= TOT, (cur[0], TOT)
    assert TOT <= 229376 - 64, TOT
    print('SBUF bytes/partition used', TOT)

    ps = nc.alloc_psum_tensor("ps", [128, 8, 512], F32)
    bank_rr = [0]

    nbank = [8]

    def bank(n=1):
        b = bank_rr[0] % nbank[0]
        if n == 2 and b % 2:
            b = (b + 1) % nbank[0]
        bank_rr[0] = (b + n) % nbank[0]
        return b

    def mm(out, lhsT, rhs, start=True, stop=True):
        P.op("pe", lambda e: e.matmul(out, lhsT=lhsT, rhs=rhs, start=start, stop=stop), [lhsT, rhs], [out])

    def tr(out, in_, ident):
        P.op("pe", lambda e: e.transpose(out=out, in_=in_, identity=ident), [in_, ident], [out])

    def act(out, in_, func, bias=None, scale=None):
        kw = {}
        reads = [in_]
        if bias is not None:
            kw["bias"] = bias
            if not isinstance(bias, float):
                reads.append(bias)
        if scale is not None:
            kw["scale"] = scale
            if not isinstance(scale, float):
                reads.append(scale)
        P.op("act", lambda e: e.activation(out=out, in_=in_, func=func, **kw), reads, [out])

    def tt(eng, out, in0, in1, op):
        P.op(eng, lambda e: e.tensor_tensor(out=out, in0=in0, in1=in1, op=op), [in0, in1], [out])

    def ts(eng, out, in0, s1, s2, op0, op1):
        reads = [in0] + [x for x in (s1, s2) if x is not None and not isinstance(x, (float, int))]
        P.op(eng, lambda e: e.tensor_scalar(out=out, in0=in0, scalar1=s1, scalar2=s2, op0=op0, op1=op1), reads, [out])

    def stt(eng, out, in0, scalar, in1, op0, op1):
        reads = [in0, in1] + ([] if isinstance(scalar, (float, int)) else [scalar])
        P.op(eng, lambda e: e.scalar_tensor_tensor(out=out, in0=in0, scalar=scalar, in1=in1, op0=op0, op1=op1), reads, [out])

    def rsq(out, in_, scale, eps, post=1.0):
        act(out, in_, AF.Ln, bias=float(eps), scale=float(scale))
        if post == 1.0:
            act(out, out, AF.Exp, scale=-0.5)
        else:
            act(out, out, AF.Exp, scale=-0.5, bias=float(np.log(post)))

    def cp(eng, out, in_):
        if eng == "act":
            act(out, in_, AF.Copy)
        else:
            P.op(eng, lambda e: e.tensor_copy(out=out, in_=in_), [in_], [out])

    def mset(eng, out, val):
        P.op(eng, lambda e: e.memset(out, val), [], [out])

    def dma(q, out, in_, rk=None, wk=None, slow=False):
        reads = [rk if rk is not None else in_]
        writes = [wk if wk is not None else out]
        if isinstance(rk, list):
            reads = rk
        if slow:
            P.dma(q, lambda e: e.dma_start(out=out, in_=in_, allow_slow_non_contiguous=True), reads, writes)
        else:
            P.dma(q, lambda e: e.dma_start(out=out, in_=in_), reads, writes)

    inst = [0]

    class _Vec:
        def __getitem__(self, key):
            rows, cols = key
            lo, hi = cols.start, cols.stop
            if hi <= 56:
                return vecG[inst[0] % 2][rows, lo:hi]
            assert lo >= 56
            return vecM[rows, lo - 56:hi - 56]
    vec = _Vec()

    def load_vec_g(l_next):
        dma("sp", vecG[(inst[0] + 1) % 2][:, :], vecs[l_next, :, 0:56], rk=("in", "vecs"))

    def load_vec_m(l_next):
        dma("sp", vecM[:, :], vecs[l_next, :, 56:NV], rk=("in", "vecs"))

    identf, U2f, U1f, onesf = cf[:, 0, :], cf[:, 1, :], cf[:, 2, :], cf[:, 3, :]
    identb, onesb, U1b = cb[:, 0, :], cb[:, 1, :], cb[:, 14, :]

    def band(kind, g):
        return cb[:, 2 + kind * 4 + g, :]

    st0 = sst
    dma("sp", st0[:, 0:17 * 128], cst.rearrange("p a b -> p (a b)"), rk=("in", "cst"))
    cp("dve", cf[:, :, :], st0[:, 0:4 * 128].rearrange("p (a b) -> p a b", b=128))
    cp("dve", cb[:, 0, :], st0[:, 0:128])
    cp("dve", cb[:, 1, :], st0[:, 3 * 128:4 * 128])
    cp("dve", cb[:, 2:14, :], st0[:, 5 * 128:17 * 128].rearrange("p (a b) -> p a b", b=128))
    cp("dve", cb[:, 14, :], st0[:, 2 * 128:3 * 128])
    LTf = st0[:, 4 * 128:5 * 128]
    pwv = sst[:, 2176:2176 + DEPTH * 512].rearrange("p (a b) -> p a b", b=128)
    dma("sp", pwv, pool_w.rearrange("l g c d -> c (l g) d"), rk=("in", "pool_w"))
    cp("act", plw[:, :, :], pwv)
    wsv = sst[:, 4224:4224 + DEPTH * 512].rearrange("p (a b) -> p a b", b=128)
    dma("sp", wsv, ws_in.rearrange("l h t s -> t (l h) s"), rk=("in", "ws"))
    tt("dve", wsv, wsv, LTf.unsqueeze(1).to_broadcast([128, DEPTH * 4, 128]), ALU.mult)
    for i in range(DEPTH * 4):
        b = bank()
        tr(ps[:, b, 0:128], wsv[:, i, :], identf)
        cp("act", wsmT[:, i, :], ps[:, b, 0:128])

    nst = {}
    jobn = [0]
    ceng = ["act", "dve"]

    gfold = sst[:, 6280:6280 + DEPTH * 8].rearrange("p (l k) -> p l k", k=8)
    dma("sp", gfold, vecs[:, :, 48:56].rearrange("l p k -> p l k"), rk=("in", "vecs"), slow=True)

    def precast(src, W, stores, name, scale=None):
        i = jobn[0] % NSTG
        jobn[0] += 1
        dma("sp", stg32[i][:, 0:W], src, rk=("in", name))
        if scale is not None:
            act(stgbf[i][:, 0:W], stg32[i][:, 0:W], AF.Copy, scale=scale)
        else:
            cp(ceng[jobn[0] % 2], stgbf[i][:, 0:W], stg32[i][:, 0:W])
        for (dst, sview) in stores:
            k = nst.get(name, 0)
            nst[name] = k + 1
            dma("act", dst, sview, wk=("scr", name, k), slow=True)

    def sv(i_, lo, n, inner):
        return stgbf[i_][:, lo:lo + n * inner].rearrange("p (a b) -> p a b", b=inner)

    for l in range(DEPTH):
        for f in range(2):
            nm = "gu%d_%d" % (f, l)
            for kc in range(8):
                for half in range(2):
                    i_ = jobn[0] % NSTG
                    precast(w_gu[f][l, kc * 128:(kc + 1) * 128, half * DFF:(half + 1) * DFF], DFF,
                            [(gu_s[f][l, :, :, kc, half * 128:(half + 1) * 128].rearrange("j p c -> p j c"),
                              sv(i_, 0, 22, 128))], nm)
            nm = "dn%d_%d" % (f, l)
            for kc in range(22):
                i_ = jobn[0] % NSTG
                precast(w_dn[f][l, kc * 128:(kc + 1) * 128, :], D,
                        [(dn_s[f][l, :, :, kc, :].rearrange("c p d -> p c d"), sv(i_, 0, 8, 128))], nm)
        for kc in range(8):
            rows = slice(kc * 128, (kc + 1) * 128)
            q, r = kc // 2, kc % 2
            i_ = jobn[0] % NSTG
            precast(w_in[l, rows, 0:512], 512, [(wtm_s[l, q, :, r, 0:512], stgbf[i_][:, 0:512])], "wtm_%d" % l)
            i_ = jobn[0] % NSTG
            precast(w_in[l, rows, 512:3584], 3072,
                    [(wfm_s[l, 0:12, :, kc, :].rearrange("q p c -> p q c"), sv(i_, 0, 12, 256))], "wfm_%d" % l)
            i_ = jobn[0] % NSTG
            precast(w_in[l, rows, 3584:3600], 16, [(wtm_s[l, q, :, r, 1024:1040], stgbf[i_][:, 0:16])], "wtm_%d" % l)
            i_ = jobn[0] % NSTG
            precast(w_in[l, rows, 3600:4112], 512, [(wfm_s[l, 12:14, :, kc, :].rearrange("q p c -> p q c"), sv(i_, 0, 2, 256))], "wfm_%d" % l)
            i_ = jobn[0] % NSTG
            precast(w_in[l, rows, 4112:4624], 512, [(wtm_s[l, q, :, r, 512:1024], stgbf[i_][:, 0:512])], "wtm_%d" % l)
            i_ = jobn[0] % NSTG
            precast(w_in[l, rows, 4624:7696], 3072,
                    [(wmg_s[l, :, :, 16 + b_ * 8 + kc, :].rearrange("c p d -> p c d"), sv(i_, b_ * 1024, 8, 128))
                     for b_ in range(3)], "wmg_%d" % l)
        for (wsrc, nk, blk0) in ((w_a, 4, 0), (w_b, 8, 4), (w_c, 4, 12)):
            for kc in range(nk):
                i_ = jobn[0] % NSTG
                precast(wsrc[l, kc * 128:(kc + 1) * 128, :], D,
                        [(wmg_s[l, :, :, blk0 + kc, :].rearrange("c p d -> p c d"), sv(i_, 0, 8, 128))], "wmg_%d" % l,
                        scale=(sst[:, 6280 + l * 8 + kc:6280 + l * 8 + kc + 1] if blk0 == 4 else None))
        for kc in range(8):
            i_ = jobn[0] % NSTG
            precast(w_o[l, kc * 128:(kc + 1) * 128, :], D,
                    [(wo_s[l, q_, :, :, kc, :].rearrange("p c d -> p c d"), sv(i_, q_ * 512, 4, 128)) for q_ in range(2)],
                    "wo_%d" % l)

    slot_rr = [0]

    def piece(src, n, name):
        s_ = slots[slot_rr[0] % NSLOT]
        slot_rr[0] += 1
        dma("sp", s_[:, 0:n], src, rk=[("scr", name, k) for k in range(nst[name])])
        return s_

    def rms_stats(T, src_chunks, sq_dst):
        b = bank()
        n = len(src_chunks)
        for i, c in enumerate(src_chunks):
            act(sq_dst(i), c, AF.Square)
            mm(ps[:, b, 0:T], onesb, sq_dst(i), start=(i == 0), stop=(i == n - 1))
        rsq(rstd[:, 0:T], ps[:, b, 0:T], 1.0 / D, EPS)

    def prenorm(T, gcol, hsrc=None):
        hsrc = h if hsrc is None else hsrc
        rms_stats(T, [hsrc[:, kc, 0:T] for kc in range(8)], lambda i: xn[:, i, 0:T])
        for kc in range(8):
            stt("dve", xn[:, kc, 0:T], hsrc[:, kc, 0:T],
                vec[:, gcol + kc:gcol + kc + 1], rstd[:, 0:T], ALU.mult, ALU.mult)

    def ffn(l, f, T, hdst=None, hsrc=None, pre_hook=None):
        hdst = h if hdst is None else hdst
        hsrc = h if hsrc is None else hsrc
        gpre = 0 if f == 0 else 32
        gpost = 8 if f == 0 else 40
        P.tag = 'ffn.pre'
        if pre_hook is not None:
            pre_hook()
        prenorm(T, gpre, hsrc)
        P.tag = 'ffn.gu'
        for j in range(22):
            w = piece(gu_s[f][l, j].rearrange("p k c -> p (k c)"), 2048, "gu%d_%d" % (f, l))
            wv = w[:, 0:2048].rearrange("p (k c) -> p k c", c=256)
            bg, bu = bank(), bank()
            for kc in range(8):
                mm(ps[:, bg, 0:T], wv[:, kc, 0:128], xn[:, kc, 0:T], kc == 0, kc == 7)
            for kc in range(8):
                mm(ps[:, bu, 0:T], wv[:, kc, 128:256], xn[:, kc, 0:T], kc == 0, kc == 7)
            act(tA[:, 0:T], ps[:, bg, 0:T], AF.Silu)
            tt("dve", hid[:, j, 0:T], tA[:, 0:T], ps[:, bu, 0:T], ALU.mult)
        P.tag = 'ffn.dn'
        act(tC[:, 0:1], onesf[:, 0:1], AF.Ln)
        bst = bank()
        for c in range(9):
            if c < 8:
                w = piece(dn_s[f][l, c].rearrange("p k d -> p (k d)"), 22 * 128, "dn%d_%d" % (f, l))
                wv = w[:, 0:22 * 128].rearrange("p (k d) -> p k d", d=128)
                b = bank()
                if b == bst:
                    b = bank()
                for kc in range(22):
                    mm(ps[:, b, 0:T], wv[:, kc, :], hid[:, kc, 0:T], kc == 0, kc == 21)
                act(fsb[:, c, 0:T], ps[:, b, 0:T], AF.Copy, scale=vec[:, gpost + c:gpost + c + 1])
                act(xn[:, c, 0:T], ps[:, b, 0:T], AF.Square)
            if c > 0:
                mm(ps[:, bst, 0:T], onesb, xn[:, c - 1, 0:T], c == 1, c == 8)
        P.tag = 'ffn.post'
        rsq(rstd[:, 0:T], ps[:, bst, 0:T], 1.0 / D, EPS, post=0.5)
        for c in range(8):
            tq = (tA, tB, tC)[c % 3]
            tt("dve", tq[:, 0:T], fsb[:, c, 0:T], rstd[:, 0:T], ALU.mult)
            tt("pool" if c in (2, 5) else "dve", hdst[:, c, 0:T], tq[:, 0:T], hsrc[:, c, 0:T], ALU.add)

    V_PS, V_CW, V_CB, V_DS, V_DTB, V_AL, V_GG, V_GB, V_BS, V_DB = 56, 60, 124, 140, 148, 164, 180, 692, 1204, 1716

    def mixer(l, T, CS, first, last, which, nxt):
        NCH = T // CS
        if nxt is not None:
            load_vec_g(nxt)
        P.tag = 'mx.pre'
        prenorm(T, 16)
        P.tag = 'mx.tm'
        act(tC[:, 0:16], vec[:, V_AL:V_AL + 16], AF.Exp)
        wq = []
        for q in range(4):
            w = piece(wtm_s[l, q].rearrange("p k c -> p (k c)"), 2 * 1040, "wtm_%d" % l)
            wq.append(w[:, 0:2080].rearrange("p (k c) -> p k c", c=1040))
        for c in range(NCH):
            tk = slice(c * CS, (c + 1) * CS)
            bxa, bv, bdt = bank(), bank(), bank()
            for (bb, c0, cn) in ((bxa, 0, 512), (bv, 512, 512), (bdt, 1024, 16)):
                for kc in range(8):
                    mm(ps[0:CS, bb, 0:cn], xn[:, kc, tk], wq[kc // 2][:, kc % 2, c0:c0 + cn], kc == 0, kc == 7)
            cp("act", xatm[0:CS, c, :], ps[0:CS, bxa, :])
            if c == NCH - 1:
                cp("act", xalast[0:CS, :], ps[0:CS, bxa, :])
            P.op("dve", (lambda bv_=bv: lambda e: e.bn_stats(out=st6[0:CS, 0:6], in_=ps[0:CS, bv_, :]))(),
                 [ps[0:CS, bv, :]], [st6[0:CS, 0:6]])
            P.op("dve", lambda e: e.bn_aggr(out=st6[0:CS, 6:8], in_=st6[0:CS, 0:6]), [st6[0:CS, 0:6]], [st6[0:CS, 6:8]])
            rsq(st6[0:CS, 7:8], st6[0:CS, 7:8], 1.0, EPS)
            ts("dve", st6[0:CS, 6:7], st6[0:CS, 6:7], st6[0:CS, 7:8], -1.0, ALU.mult, ALU.mult)
            act(vn32[0:CS, :], ps[0:CS, bv, :], AF.Identity, bias=st6[0:CS, 6:7], scale=st6[0:CS, 7:8])
            tt("dve", vn32[0:CS, :], vn32[0:CS, :], vec[0:CS, V_GG:V_GG + 512], ALU.mult)
            tt("dve", vn32[0:CS, :], vn32[0:CS, :], vec[0:CS, V_GB:V_GB + 512], ALU.add)
            cp("act", vnb[0:CS, c, :], vn32[0:CS, :])
            if which == 1:
                dma("act", o_gv[l], vn32[0:16, :], wk=("out", "gv", l))
            tt("dve", dtb[0:CS, c, :], ps[0:CS, bdt, 0:16], vec[0:CS, V_DTB:V_DTB + 16], ALU.add)
            act(dtb[0:CS, c, :], dtb[0:CS, c, :], AF.Exp)
            act(dtb[0:CS, c, :], dtb[0:CS, c, :], AF.Ln, bias=1.0)
            stt("dve", dtab[0:CS, c, :], dtb[0:CS, c, :], -1.0, tC[0:CS, 0:16], ALU.mult, ALU.mult)
        if last:
            if which == 0:
                dma("act", o_pool[0][l], xalast[113:128, :], wk=("out", "pool", l))
            else:
                dma("act", o_pool[1][l], xalast[1:16, :], wk=("out", "pool", l))
        P.tag = 'mx.pool'
        for c in range(NCH):
            b = bank()
            for g in range(4):
                gs = slice(g * 128, (g + 1) * 128)
                mm(ps[:, b, g * 128:g * 128 + CS], xatm[0:CS, c, gs], band(0 if (first and c == 0 and which == 0) else 1, g)[0:CS, 0:CS],
                   True, False)
                if c == 0:
                    mm(ps[:, b, g * 128:g * 128 + CS], xac[l][64:128, gs], band(2, g)[64:128, 0:CS], False, True)
                else:
                    mm(ps[:, b, g * 128:g * 128 + CS], xatm[64:128, c - 1, gs], band(2, g)[64:128, 0:CS], False, True)
            cp("act", zz[:, :, c * CS:(c + 1) * CS], ps[:, b, :].rearrange("p (g t) -> p g t", t=128)[:, :, 0:CS])
        cp("pool", xac[l][:, :], xatm[:, NCH - 1, :]) if CS == 128 else None
        Mbs = [Mb, Mb2]

        def S3(c):
            tk = slice(c * CS, (c + 1) * CS)
            xdt, Btm, xD = xdts[c % 2], Btms[c % 2], xDs[c % 2]
            b = bank()
            pb = ps[:, b, :].bitcast(BF16)
            for j in range(8):
                tr(pb[0:CS, j * 128:(j + 1) * 128], xs[:, j, tk], identb)
            b2 = bank()
            pb2 = ps[:, b2, :].bitcast(BF16)
            for g in range(4):
                tr(pb2[0:CS, g * 128:(g + 1) * 128], BC[:, g, tk], identb)
            dtv = dtb[0:CS, c, :].unsqueeze(2).to_broadcast([CS, 16, 64])
            p3 = pb[0:CS, 0:1024].rearrange("p (h q) -> p h q", q=64)
            tt("dve", xdt[0:CS, :].rearrange("p (h q) -> p h q", q=64), p3, dtv, ALU.mult)
            tt("dve", xD[0:CS, :].rearrange("p (h q) -> p h q", q=64), p3,
               vec[0:CS, V_DB:V_DB + 16].unsqueeze(2).to_broadcast([CS, 16, 64]), ALU.mult)
            cp("act", Btm[0:CS, :], pb2[0:CS, 0:512])

        def S1(c):
            tk = slice(c * CS, (c + 1) * CS)
            dta = dtab[0:CS, c, :]
            exb = exbs[c % 2]
            xdt, xw = xdts[c % 2], xws[c % 2]
            b = bank()
            mm(ps[0:CS, b, 0:16], U2f[0:CS, 0:CS], dta)
            mm(ps[0:CS, b, 16:32], U1f[0:CS, 0:CS], dta)
            mm(ps[:, b, 32:48], onesf[0:CS, :], dta)
            b2 = bank()
            for g in range(4):
                mm(ps[0:CS, b2, g * 128:g * 128 + CS], BC[:, g, tk], BC[:, 4 + g, tk])
            act(exb[0:CS, 0:32], ps[0:CS, b, 0:32], AF.Exp)
            act(exb[:, 32:48], ps[:, b, 32:48], AF.Exp)
            tt("dve", dtaU[0:CS, :, 0:CS], dta.unsqueeze(2).to_broadcast([CS, 16, CS]),
               U2f[0:CS, 0:CS].unsqueeze(1).to_broadcast([CS, 16, CS]), ALU.mult)
            tt("dve", cbm[0:CS, :, 0:CS], ps[0:CS, b2, :].rearrange("p (g t) -> p g t", t=128)[:, :, 0:CS],
               U2f[0:CS, 0:CS].unsqueeze(1).to_broadcast([CS, 4, CS]), ALU.mult)
            tt("pool", xw[0:CS, :].rearrange("p (h q) -> p h q", q=64), xdt[0:CS, :].rearrange("p (h q) -> p h q", q=64),
               exb[0:CS, 16:32].unsqueeze(2).to_broadcast([CS, 16, 64]), ALU.mult)

        def S2(c):
            Mc = Mbs[c % 2]
            for hf in range(2):
                b = bank(2)
                for q4 in range(2):
                    hq = hf * 2 + q4
                    if CS == 128:
                        mm(ps[0:CS, b + q4, :], U1b[0:CS, 0:CS], dtaU[0:CS, hq * 4:(hq + 1) * 4, :].rearrange("p a b -> p (a b)"))
                    else:
                        for hh in range(4):
                            mm(ps[0:CS, b + q4, hh * 128:hh * 128 + CS], U1b[0:CS, 0:CS], dtaU[0:CS, hq * 4 + hh, 0:CS])
                act(Eb[0:CS, hf * 8:(hf + 1) * 8, 0:CS],
                    ps[0:CS, b:b + 2, :].rearrange("p a (h t) -> p (a h) t", t=128)[:, :, 0:CS], AF.Exp)
            tt("dve", Mc[0:CS, :, 0:CS].rearrange("p (g r) t -> p g r t", r=4),
               Eb[0:CS, :, 0:CS].rearrange("p (g r) t -> p g r t", r=4),
               cbm[0:CS, :, 0:CS].unsqueeze(2).to_broadcast([CS, 4, 4, CS]), ALU.mult)

        stb = {}

        def S4(c):
            tk = slice(c * CS, (c + 1) * CS)
            exb = exbs[c % 2]
            Mc, xdt, xw, Btm, xD = Mbs[c % 2], xdts[c % 2], xws[c % 2], Btms[c % 2], xDs[c % 2]
            ecum = exb[0:CS, 0:16]
            by = bank(2)
            for hb in range(2):
                mm(ps[0:CS, by + hb, :], identb[0:CS, 0:CS], xD[0:CS, hb * 512:(hb + 1) * 512], True, False)
            for hh in range(16):
                mm(ps[0:CS, by + hh // 8, (hh % 8) * 64:(hh % 8) * 64 + 64], Mc[0:CS, hh, 0:CS], xdt[0:CS, hh * 64:(hh + 1) * 64],
                   False, hh % 8 == 7)
            bo = bank(2)
            for g in range(4):
                mm(ps[0:CS, bo + g // 2, (g % 2) * 256:(g % 2) * 256 + 256], BC[:, 4 + g, tk], Sbf1[:, g * 256:(g + 1) * 256])
            bs_ = 6
            for g in range(4):
                mm(ps[:, bs_ + g // 2, (g % 2) * 256:(g % 2) * 256 + 256], Btm[0:CS, g * 128:(g + 1) * 128],
                   xw[0:CS, g * 256:(g + 1) * 256])
            stb[c] = bs_
            pyo = ps[0:CS, bo:bo + 2, :].rearrange("p a (h q) -> p (a h) q", q=64)
            pyd = ps[0:CS, by:by + 2, :].rearrange("p a (h q) -> p (a h) q", q=64)
            y3 = ytm[0:CS, :].rearrange("p (h q) -> p h q", q=64)
            tt("dve", y3, pyo, ecum.unsqueeze(2).to_broadcast([CS, 16, 64]), ALU.mult)
            tt("dve", y3, y3, pyd, ALU.add)

        def S5(c):
            tk = slice(c * CS, (c + 1) * CS)
            byt = bank(2)
            for j in range(8):
                tr(ps[:, byt + j // 4, (j % 4) * 128:(j % 4) * 128 + CS], ytm[0:CS, j * 128:(j + 1) * 128], identf[0:CS, 0:CS])
            pyt = ps[:, byt:byt + 2, :].rearrange("p a (j t) -> p (a j) t", t=128)[:, :, 0:CS]
            elast = exbs[c % 2][:, 32:48]
            s3 = S32[l][:, :].rearrange("p (h q) -> p h q", q=64)
            tt("pool", s3, s3, elast.unsqueeze(2).to_broadcast([128, 16, 64]), ALU.mult)
            tt("dve", ygb[:, :, 0:CS], pyt, sz[:, :, tk], ALU.mult)
            act(sqb[:, :, 0:CS], ygb[:, :, 0:CS], AF.Square)
            bs_ = stb[c]
            tt("dve", S32[l][:, :].rearrange("p (a n) -> p a n", a=2), S32[l][:, :].rearrange("p (a n) -> p a n", a=2),
               ps[:, bs_:bs_ + 2, :], ALU.add)
            cp("act", Sbf1[:, :], S32[l][:, :])

        def S6(c):
            tk = slice(c * CS, (c + 1) * CS)
            elast = exbs[c % 2][:, 32:48]
            b = bank()
            for j in range(8):
                mm(ps[:, b, 0:CS], onesb, sqb[:, j, 0:CS], j == 0, j == 7)
            rsq(rsb[:, 0:CS], ps[:, b, 0:CS], 1.0 / D, EPS)
            tt("dve", yb[:, :, tk], ygb[:, :, 0:CS], rsb[:, 0:CS].unsqueeze(1).to_broadcast([128, 8, CS]), ALU.mult)

        ssd_pre = [None]

        def _ssd_pre():
            cp("act", Sbf1[:, :], S32[l][:, :])
            nbank[0] = 6
            S3(0)
            S1(0)
        ssd_pre[0] = _ssd_pre
        P.tag = 'mx.fm'
        accs = [tA, tB, tC]
        pend = []
        for pi in (12, 13, 4, 5, 6, 7, 8, 9, 10, 11, 0, 1, 2, 3):
            w = piece(wfm_s[l, pi].rearrange("p k c -> p (k c)"), 2048, "wfm_%d" % l)
            wv = w[:, 0:2048].rearrange("p (k c) -> p k c", c=256)
            for ci in range(2):
                b = bank()
                for kc in range(8):
                    mm(ps[:, b, 0:T], wv[:, kc, ci * 128:(ci + 1) * 128], xn[:, kc, 0:T], kc == 0, kc == 7)
                if pi >= 12:
                    cp("act", ubuf[:, (pi - 12) * 2 + ci, 0:T], ps[:, b, 0:T])
                elif pi < 4:
                    act(sz[:, pi * 2 + ci, 0:T], ps[:, b, 0:T], AF.Silu)
                else:
                    j = (pi - 4) * 2 + ci
                    ext = exts[j % 3]
                    acc = accs[j % 3]
                    cp("pool", ext[:, 0:3], hist[l][:, j, :])
                    cp("act", ext[:, 3:3 + T], ps[:, b, 0:T])
                    cp("pool", hist[l][:, j, :], ext[:, T:T + 3])
                    cw = V_CW + j * 4
                    act(acc[:, 0:T], ext[:, 0:T], AF.Identity, bias=vec[:, V_CB + j:V_CB + j + 1], scale=vec[:, cw:cw + 1])
                    stt("dve", acc[:, 0:T], ext[:, 1:1 + T], vec[:, cw + 1:cw + 2], acc[:, 0:T], ALU.mult, ALU.add)
                    stt("dve", acc[:, 0:T], ext[:, 2:2 + T], vec[:, cw + 2:cw + 3], acc[:, 0:T], ALU.mult, ALU.add)
                    stt("dve", acc[:, 0:T], ext[:, 3:3 + T], vec[:, cw + 3:cw + 4], acc[:, 0:T], ALU.mult, ALU.add)
                    dst = xs[:, j, 0:T] if j < 8 else BC[:, j - 8, 0:T]
                    for (d_, a_) in pend:
                        act(d_, a_, AF.Silu)
                    pend = [(dst, acc[:, 0:T])]
        for (d_, a_) in pend:
            act(d_, a_, AF.Silu)
        for g in range(4):
            b = bank()
            mm(ps[:, b, 0:T], plw[:, l * 4 + g, :], zz[:, g, 0:T])
            act(ya[:, g, 0:T], ps[:, b, 0:T], AF.Copy, scale=vec[:, V_PS + g:V_PS + g + 1])
        ssd_pre[0]()
        for c in range(NCH):
            b = bank()
            for hd in range(4):
                mm(ps[:, b, hd * 128:hd * 128 + CS], vnb[0:CS, c, hd * 128:(hd + 1) * 128],
                   wsmT[0:CS, l * 4 + hd, 0:CS])
            pv = ps[:, b, :].rearrange("p (g t) -> p g t", t=128)[:, :, 0:CS]
            tv = tA[:, 0:512].rearrange("p (g t) -> p g t", t=128)[:, :, 0:CS]
            tt("dve", tv, pv, vec[:, V_BS:V_BS + 512].rearrange("p (g t) -> p g t", t=128)[:, :, 0:CS], ALU.add)
            tt("dve", yc[:, :, c * CS:(c + 1) * CS], tv, ubuf[:, :, c * CS:(c + 1) * CS], ALU.mult)
        P.tag = 'mx.ssd'
        S2(0)
        for c in range(NCH):
            S4(c)
            if c + 1 < NCH:
                S3(c + 1)
                S1(c + 1)
            S5(c)
            if c + 1 < NCH:
                S2(c + 1)
            S6(c)
        nbank[0] = 8
        if CS < 128:
            pass
        if last:
            dma("act", o_conv[which][l], hist[l][:, :, :], wk=("out", "conv", l))
            dma("act", o_ssm[which][l], S32[l][:, :], wk=("out", "ssm", l))
        P.tag = 'mx.merge'
        bst = bank()
        for c in range(8):
            w0 = piece(wmg_s[l, c, :, 0:20, :].rearrange("p k d -> p (k d)"), 2560, "wmg_%d" % l)
            w1 = piece(wmg_s[l, c, :, 20:40, :].rearrange("p k d -> p (k d)"), 2560, "wmg_%d" % l)
            wv0 = w0[:, 0:2560].rearrange("p (k d) -> p k d", d=128)
            wv1 = w1[:, 0:2560].rearrange("p (k d) -> p k d", d=128)

            def wvk(k_):
                return wv0[:, k_, :] if k_ < 20 else wv1[:, k_ - 20, :]
            banks = []
            for _ in range(6):
                b = bank()
                if b == bst:
                    b = bank()
                banks.append(b)
            for kc in range(4):
                mm(ps[:, banks[0], 0:T], wvk(kc), ya[:, kc, 0:T], kc == 0, kc == 3)
            for kc in range(8):
                mm(ps[:, banks[1], 0:T], wvk(4 + kc), yb[:, kc, 0:T], kc == 0, kc == 7)
            for kc in range(4):
                mm(ps[:, banks[2], 0:T], wvk(12 + kc), yc[:, kc, 0:T], kc == 0, kc == 3)
            for b_ in range(3):
                for kc in range(8):
                    mm(ps[:, banks[3 + b_], 0:T], wvk(16 + b_ * 8 + kc), xn[:, kc, 0:T], kc == 0, kc == 7)
                act(sg[:, b_, 0:T], ps[:, banks[3 + b_], 0:T], AF.Sigmoid)
            tt("dve", sg[:, 0, 0:T], sg[:, 0, 0:T], ps[:, banks[0], 0:T], ALU.mult)
            tt("dve", sg[:, 1, 0:T], sg[:, 1, 0:T], ps[:, banks[1], 0:T], ALU.mult)
            tt("dve", sg[:, 2, 0:T], sg[:, 2, 0:T], ps[:, banks[2], 0:T], ALU.mult)
            tt("pool", sg[:, 0, 0:T], sg[:, 0, 0:T], sg[:, 1, 0:T], ALU.add)
            tt("pool", mrg[:, c, 0:T], sg[:, 0, 0:T], sg[:, 2, 0:T], ALU.add)
        P.tag = 'mx.out'
        if nxt is not None:
            load_vec_m(nxt)
        wos = []
        for q in range(4):
            w = piece(wo_s[l, q // 2, :, (q % 2) * 2:(q % 2) * 2 + 2].rearrange("p c k d -> p (c k d)"), 2048, "wo_%d" % l)
            wos.append(w[:, 0:2048].rearrange("p (c k d) -> p c k d", k=8, d=128))
        act(tC[:, 0:1], onesf[:, 0:1], AF.Ln)
        bst = bank()
        for c in range(9):
            if c < 8:
                b = bank()
                if b == bst:
                    b = bank()
                for kc in range(8):
                    mm(ps[:, b, 0:T], wos[c // 2][:, c % 2, kc, :], mrg[:, kc, 0:T], kc == 0, kc == 7)
                act(msb[:, c, 0:T], ps[:, b, 0:T], AF.Copy, scale=vec[:, 24 + c:24 + c + 1])
                act(xn[:, c, 0:T], ps[:, b, 0:T], AF.Square)
            if c > 0:
                mm(ps[:, bst, 0:T], onesb, xn[:, c - 1, 0:T], c == 1, c == 8)
        rsq(rstd[:, 0:T], ps[:, bst, 0:T], 1.0 / D, EPS)
        for c in range(8):
            tq = (tA, tB, tC)[c % 3]
            tt("dve", tq[:, 0:T], msb[:, c, 0:T], rstd[:, 0:T], ALU.mult)
            tt("dve", h[:, c, 0:T], tq[:, 0:T], h[:, c, 0:T], ALU.add)

    def run_tile(T, CS, first, last, which, src, dst, final, pref, nxt_src):
        if not pref:
            dma("sp", h[:, :, 0:T], src, rk=("in", "x"))
        for l in range(DEPTH):
            if inst[0] == 0:
                dma("sp", vecG[0][:, :], vecs[0, :, 0:56], rk=("in", "vecs"))
                load_vec_m(0)
            nxt = None if (final and l == DEPTH - 1) else (l + 1) % DEPTH
            if first:
                if which == 0:
                    mset("dve", S32[l][:, :], 0.0)
                    mset("dve", hist[l][:, :, :], 0.0)
                    mset("dve", xac[l][:, :], 0.0)
                else:
                    dma("sp", S32[l][:, :], ssm_st[l], rk=("in", "ssm"))
                    dma("sp", hist[l][:, :, :], conv_st[l], rk=("in", "conv"))
                    mset("dve", vn32[:, :], 0.0)
                    dma("sp", vn32[113:128, :], pool_st[l], rk=("in", "poolst"))
                    cp("act", xac[l][:, :], vn32[:, :])
            ffn(l, 0, T, hsrc=(xpre if (pref and l == 0) else None))
            mixer(l, T, CS, first, last, which, nxt)
            hook = None
            if l == DEPTH - 1 and nxt_src is not None:
                def hook(ns=nxt_src):
                    dma("sp", xpre[:, :, 0:ns[1]], ns[0], rk=("in", "x"))
            ffn(l, 1, T, hdst=(msb if l == DEPTH - 1 else None), pre_hook=hook)
            inst[0] += 1
        dma("act", dst, msb[:, :, 0:T], wk=("out", "y", which, id(dst)))

    for t in range(NT):
        nsrc = (xT[t + 1], 512) if t + 1 < NT else ((xsT, 16) if has_sample else None)
        run_tile(512, 128, t == 0, t == NT - 1, 0, xT[t], yT[t], (not has_sample) and t == NT - 1, t > 0, nsrc)
    if has_sample:
        run_tile(16, 16, True, True, 1, xsT, ysT, True, True, None)

    eng_obj = {"pe": nc.tensor, "act": nc.scalar, "dve": nc.vector, "pool": nc.gpsimd, "sp": nc.sync}
    names = ["pe", "act", "dve", "pool"] + ["d%d" % i for i in range(NDS)]
    import contextlib
    with contextlib.ExitStack() as es_:
        sems = {n: es_.enter_context(nc.semaphore("s_" + n)) for n in names}
        block = es_.enter_context(nc.Block())
        fin = [("d%d" % i, P.dma_tot[i]) for i in range(NDS) if P.dma_tot[i] > 0]

        def replay(name, e, tail=False):
            for waits, fn, sk, inc in P.q[name]:
                for (wsk, v) in waits:
                    e.wait_ge(sems[wsk], v)
                fn(e).then_inc(sems[sk], inc)
            if tail:
                for (wsk, v) in fin:
                    e.wait_ge(sems[wsk], v)

        @block.sync
        def _(e):
            replay("sp", e)

        @block.tensor
        def _(e):
            replay("pe", e)

        @block.scalar
        def _(e):
            replay("act", e, tail=True)

        @block.vector
        def _(e):
            replay("dve", e)

        @block.gpsimd
        def _(e):
            replay("pool", e)
    return nc, {e: len(P.q[e]) for e in P.q}, P.tags


def make_consts():
    k = np.arange(128)[:, None]
    t = np.arange(128)[None, :]
    c = np.zeros((128, 17, 128), np.float32)
    c[:, 0] = (k == t)
    c[:, 1] = (k <= t)
    c[:, 2] = (k > t)
    c[:, 3] = 1.0
    c[:, 4] = (k >= t)
    for g, w in enumerate((2, 4, 8, 16)):
        inwin = (k <= t) & (k > t - w)
        cnt0 = np.minimum(t + 1, w).astype(np.float32)
        c[:, 5 + g] = inwin / cnt0 - (k == t)
        c[:, 9 + g] = inwin / np.float32(w) - (k == t)
        c[:, 13 + g] = ((t + 128 - k) <= (w - 1)) / np.float32(w)
    return c


def pack_vecs(inp, depth):
    v = np.zeros((depth, 128, NV), np.float32)

    def fm(a, n):
        return np.transpose(a.reshape(depth, n, 128), (0, 2, 1))
    o = 0
    for nm in ("ffn1_pre_g", "ffn1_post_g", "mix_pre_g", "mix_post_g", "ffn2_pre_g", "ffn2_post_g", "ssm_norm_g"):
        v[:, :, o:o + 8] = fm(inp[nm], 8)
        o += 8
    v[:, :, 56:60] = fm(inp["pool_scale"], 4)
    cw = inp["ssm_conv_w"].reshape(depth, 4, 16, 128)
    v[:, :, 60:124] = np.transpose(cw, (0, 3, 2, 1)).reshape(depth, 128, 64)
    v[:, :, 124:140] = fm(inp["ssm_conv_b"], 16)
    ds = inp["ssm_d"].reshape(depth, 8, 2)
    v[:, :, 140:148] = np.repeat(np.transpose(ds, (0, 2, 1)), 64, axis=1)
    v[:, :, 148:164] = np.broadcast_to(inp["ssm_dt_bias"][:, None, :], (depth, 128, 16))
    v[:, :, 164:180] = np.broadcast_to(inp["ssm_a_log"][:, None, :], (depth, 128, 16))
    v[:, :, 180:692] = np.broadcast_to(inp["gmlp_norm_g"][:, None, :], (depth, 128, 512))
    v[:, :, 692:1204] = np.broadcast_to(inp["gmlp_norm_b"][:, None, :], (depth, 128, 512))
    v[:, :, 1204:1716] = np.broadcast_to(inp["gmlp_bs"].reshape(depth, 1, 512), (depth, 128, 512))
    v[:, :, 1716:1732] = np.broadcast_to(inp["ssm_d"][:, None, :], (depth, 128, 16))
    return v


_CACHE = {}


def run(inp, n_cores, seq, depth, runner=None):
    NT = seq // 512
    key = (NT, depth)
    if key not in _CACHE:
        _CACHE[key] = build(NT, depth)
    nc = _CACHE[key][0]
    f32 = np.float32
    vecs = pack_vecs(inp, depth)
    cst = make_consts()
    shared = {
        "w_gu1": inp["ffn1_w_gu"], "w_gu2": inp["ffn2_w_gu"], "w_dn1": inp["ffn1_w_down"], "w_dn2": inp["ffn2_w_down"],
        "w_in": inp["w_in"], "w_a": inp["w_branch_a"], "w_b": inp["w_branch_b"], "w_c": inp["w_branch_c"],
        "w_o": inp["w_out"], "vecs": vecs, "pool_w": inp["pool_w"], "ws": inp["gmlp_ws"], "cst": cst,
    }
    shared = {k: np.ascontiguousarray(v, dtype=f32) for k, v in shared.items()}
    in_maps = []
    for c in range(n_cores):
        xp = inp["x_prompt"][c].reshape(NT, 512, 8, 128)
        m = dict(shared)
        m["xT"] = np.ascontiguousarray(np.transpose(xp, (0, 3, 2, 1)), dtype=f32)
        m["xsT"] = np.ascontiguousarray(np.transpose(inp["x_sample"][c].reshape(16, 8, 128), (2, 1, 0)), dtype=f32)
        m["pool_st"] = np.ascontiguousarray(inp["state_pool"][:, c], dtype=f32)
        m["conv_st"] = np.ascontiguousarray(np.transpose(inp["state_conv"][:, c].reshape(depth, 3, 16, 128), (0, 3, 2, 1)), dtype=f32)
        m["ssm_st"] = np.ascontiguousarray(np.transpose(inp["state_ssm"][:, c].reshape(depth, 1024, 128), (0, 2, 1)), dtype=f32)
        in_maps.append(m)
    if runner is None:
        res = run_bass_kernel_spmd(nc, in_maps, core_ids=list(range(n_cores))).results
    else:
        res = runner(nc, in_maps)
    B = n_cores
    y_p = np.stack([np.transpose(r["yT"], (0, 3, 2, 1)).reshape(seq, D) for r in res])
    y_s = np.stack([np.transpose(r["ysT"], (2, 1, 0)).reshape(16, D) for r in res])

    def conv_back(a):
        return np.transpose(a, (0, 3, 2, 1)).reshape(depth, 3, 2048)

    def ssm_back(a):
        return np.transpose(a, (0, 2, 1)).reshape(depth, 16, 64, 128)
    outs = [y_p, y_s]
    for sfx in ("p", "s"):
        outs.append(np.stack([r["npool_" + sfx] for r in res], axis=1))
        outs.append(np.stack([conv_back(r["nconv_" + sfx]) for r in res], axis=1))
        outs.append(np.stack([ssm_back(r["nssm_" + sfx]) for r in res], axis=1))
    outs.append(np.stack([r["gv_s"] for r in res], axis=1))
    return tuple(np.ascontiguousarray(o, dtype=f32) for o in outs)


def kernel(**inputs):
    inp = {k: np.asarray(v) for k, v in inputs.items()}
    return run(inp, 8, 8192, 4)
```

```python
import numpy as np
import concourse.bass as bass
import concourse.mybir as mybir
from concourse.bass_utils import run_bass_kernel_spmd

F32, BF16 = mybir.dt.float32, mybir.dt.bfloat16
AF = mybir.ActivationFunctionType
ALU = mybir.AluOpType

D = 1024
DFF = 2816
INC = 7696
EPS = 1e-6
GR = 256
NDS = 8
NV = 1732


def _es(dt):
    return 2 if dt == BF16 else 4


def rng(ap):
    t = ap.tensor
    es = _es(ap.dtype)
    shp = list(t.shape)
    rowb = int(np.prod(shp[1:])) * _es(t.dtype)
    f0 = (ap.offset * es) % rowb
    ext = 1
    for st, c in list(ap.ap)[1:]:
        ext += (c - 1) * abs(st)
    if type(t).__name__.startswith("PSum"):
        return "ps", f0, f0 + ext * es, 2048
    base = t.manual_sbuf_range[0]
    return "sb", base + f0, base + f0 + ext * es, GR


class Prog:
    ENG = ["pe", "act", "dve", "pool", "sp"]

    def __init__(s):
        s.q = {e: [] for e in s.ENG}
        s.cnt = {e: 0 for e in s.ENG}
        s.waited = {e: {} for e in s.ENG}
        s.lw = {}
        s.rd = {}
        s.dma_tot = [0] * NDS
        s.dma_rr = 0
        s.tag = ''
        s.tags = {e: [] for e in s.ENG}

    def keys(s, x):
        if isinstance(x, tuple):
            return [x]
        sp, lo, hi, g = rng(x)
        return [(sp, i) for i in range(lo // g, (hi - 1) // g + 1)]

    def _deps(s, eng, reads, writes):
        need = {}

        def add(sk, v):
            if need.get(sk, 0) < v:
                need[sk] = v
        for r in reads:
            for k in s.keys(r):
                if k in s.lw:
                    add(*s.lw[k])
        for w in writes:
            for k in s.keys(w):
                if k in s.lw:
                    add(*s.lw[k])
                for sk, v in s.rd.get(k, {}).items():
                    add(sk, v)
        waits = []
        for sk, v in need.items():
            if sk == eng and eng == "pe":
                continue
            if s.waited[eng].get(sk, 0) >= v:
                continue
            s.waited[eng][sk] = v
            waits.append((sk, v))
        return waits

    def _commit(s, ev, reads, writes):
        for r in reads:
            for k in s.keys(r):
                d = s.rd.setdefault(k, {})
                if d.get(ev[0], 0) < ev[1]:
                    d[ev[0]] = ev[1]
        for w in writes:
            for k in s.keys(w):
                s.lw[k] = ev
                s.rd[k] = {}

    def op(s, eng, fn, reads, writes):
        waits = s._deps(eng, reads, writes)
        s.cnt[eng] += 1
        s.q[eng].append((waits, fn, eng, 1))
        s.tags[eng].append(s.tag)
        s._commit((eng, s.cnt[eng]), reads, writes)

    def dma(s, qeng, fn, reads, writes):
        waits = s._deps(qeng, reads, writes)
        i = s.dma_rr
        s.dma_rr = (i + 1) % NDS
        sk = "d%d" % i
        if s.dma_tot[i] > 0 and s.waited[qeng].get(sk, 0) < s.dma_tot[i]:
            waits.append((sk, s.dma_tot[i]))
            s.waited[qeng][sk] = s.dma_tot[i]
        s.dma_tot[i] += 16
        s.q[qeng].append((waits, fn, sk, 16))
        s.tags[qeng].append(s.tag)
        s._commit((sk, s.dma_tot[i]), reads, writes)


def build(NT, DEPTH, has_sample=True):
    nc = bass.Bass("TRN2", target_bir_lowering=False)
    P = Prog()

    def din(name, shape, dt=F32):
        return nc.dram_tensor(name, list(shape), dt, kind="ExternalInput").ap()

    def dout(name, shape):
        return nc.dram_tensor(name, list(shape), F32, kind="ExternalOutput").ap()

    def dscr(name, shape):
        return nc.dram_tensor(name, list(shape), BF16, kind="Internal").ap()

    xT = din("xT", [NT, 128, 8, 512])
    xsT = din("xsT", [128, 8, 16])
    pool_st = din("pool_st", [DEPTH, 15, 512])
    conv_st = din("conv_st", [DEPTH, 128, 16, 3])
    ssm_st = din("ssm_st", [DEPTH, 128, 1024])
    w_gu = [din("w_gu1", [DEPTH, D, 2 * DFF]), din("w_gu2", [DEPTH, D, 2 * DFF])]
    w_dn = [din("w_dn1", [DEPTH, DFF, D]), din("w_dn2", [DEPTH, DFF, D])]
    w_in = din("w_in", [DEPTH, D, INC])
    w_a = din("w_a", [DEPTH, 512, D])
    w_b = din("w_b", [DEPTH, D, D])
    w_c = din("w_c", [DEPTH, 512, D])
    w_o = din("w_o", [DEPTH, D, D])
    vecs = din("vecs", [DEPTH, 128, NV])
    pool_w = din("pool_w", [DEPTH, 4, 128, 128])
    ws_in = din("ws", [DEPTH, 4, 128, 128])
    cst = din("cst", [128, 17, 128])

    yT = dout("yT", [NT, 128, 8, 512])
    ysT = dout("ysT", [128, 8, 16])
    o_pool = [dout("npool_p", [DEPTH, 15, 512]), dout("npool_s", [DEPTH, 15, 512])]
    o_conv = [dout("nconv_p", [DEPTH, 128, 16, 3]), dout("nconv_s", [DEPTH, 128, 16, 3])]
    o_ssm = [dout("nssm_p", [DEPTH, 128, 1024]), dout("nssm_s", [DEPTH, 128, 1024])]
    o_gv = dout("gv_s", [DEPTH, 16, 512])

    gu_s = [dscr("gu1_s", [DEPTH, 22, 128, 8, 256]), dscr("gu2_s", [DEPTH, 22, 128, 8, 256])]
    dn_s = [dscr("dn1_s", [DEPTH, 8, 128, 22, 128]), dscr("dn2_s", [DEPTH, 8, 128, 22, 128])]
    wtm_s = dscr("wtm_s", [DEPTH, 4, 128, 2, 1040])
    wfm_s = dscr("wfm_s", [DEPTH, 14, 128, 8, 256])
    wmg_s = dscr("wmg_s", [DEPTH, 8, 128, 40, 128])
    wo_s = dscr("wo_s", [DEPTH, 2, 128, 4, 8, 128])

    SB0 = 20480
    cur = [SB0]

    def sb(name, shape, dt, at=None):
        nb = int(np.prod(shape[1:])) * _es(dt)
        nb = (nb + GR - 1) // GR * GR
        if at is None:
            off = cur[0]
            cur[0] += nb
        else:
            off = at
        return nc.alloc_sbuf_tensor_at(name, list(shape), dt, offset=off), off, nb

    h, _, _ = sb("h", [128, 8, 512], F32)
    S32 = [sb("S32_%d" % l, [128, 1024], F32)[0] for l in range(DEPTH)]
    Sbf1, _, _ = sb("Sbf1", [128, 1024], BF16)
    hist = [sb("hist_%d" % l, [128, 16, 3], F32)[0] for l in range(DEPTH)]
    xac = [sb("xac_%d" % l, [128, 512], BF16)[0] for l in range(DEPTH)]
    cf, _, _ = sb("cf", [128, 4, 128], F32)
    cb, _, _ = sb("cb", [128, 15, 128], BF16)
    wsmT, _, _ = sb("wsmT", [128, DEPTH * 4, 128], BF16)
    plw, _, _ = sb("plw", [128, DEPTH * 4, 128], BF16)
    vecG = [sb("vecG%d" % i, [128, 56], F32)[0] for i in range(2)]
    vecM, _, _ = sb("vecM", [128, NV - 56], F32)
    NSLOT = 6
    SLOTN = 2816
    slots = [sb("slot%d" % i, [128, SLOTN], BF16)[0] for i in range(NSLOT)]
    A0 = cur[0]
    xn, _, _ = sb("xn", [128, 8, 512], BF16)
    rstd, _, _ = sb("rstd", [128, 512], F32)
    tA, _, _ = sb("tA", [128, 512], F32)
    tB, _, _ = sb("tB", [128, 512], F32)
    tC, _, _ = sb("tC", [128, 512], F32)
    RA = cur[0]
    hid, _, _ = sb("hid", [128, 22, 512], BF16)
    fsb, _, _ = sb("fsb", [128, 8, 512], F32)
    RAend = cur[0]
    cur[0] = RA
    sz, _, _ = sb("sz", [128, 8, 512], BF16)
    xs, _, _ = sb("xs", [128, 8, 512], BF16)
    BC, _, _ = sb("BC", [128, 8, 512], BF16)
    yb, _, _ = sb("yb", [128, 8, 512], BF16)
    ya, _, _ = sb("ya", [128, 4, 512], BF16)
    assert cur[0] <= RAend
    cur[0] = RAend
    yc, _, _ = sb("yc", [128, 4, 512], BF16)
    dtb, _, _ = sb("dtb", [128, 4, 16], F32)
    dtab, _, _ = sb("dtab", [128, 4, 16], F32)
    exts = [sb("ext%d" % i, [128, 515], F32)[0] for i in range(3)]
    RC = cur[0]
    xatm, _, _ = sb("xatm", [128, 4, 512], BF16)
    xalast, _, _ = sb("xalast", [128, 512], F32)
    vnb, _, _ = sb("vnb", [128, 4, 512], BF16)
    ubuf, _, _ = sb("ubuf", [128, 4, 512], F32)
    zz, _, _ = sb("zz", [128, 4, 512], BF16)
    vn32, _, _ = sb("vn32", [128, 512], F32)
    st6, _, _ = sb("st6", [128, 8], F32)
    RC1 = cur[0]
    cur[0] = RC
    Eb, _, _ = sb("Eb", [128, 16, 128], BF16)
    Mb, _, _ = sb("Mb", [128, 16, 128], BF16)
    Mb2, _, _ = sb("Mb2", [128, 16, 128], BF16)
    ytm, _, _ = sb("ytm", [128, 1024], F32)
    ygb, _, _ = sb("ygb", [128, 8, 128], F32)
    sqb, _, _ = sb("sqb", [128, 8, 128], BF16)
    rsb, _, _ = sb("rsb", [128, 128], F32)
    assert cur[0] - RC >= 18 * 1024
    dtaU, _, _ = sb("dtaU", [128, 16, 128], BF16)
    cbm, _, _ = sb("cbm", [128, 4, 128], BF16)
    xdts = [sb("xdt%d" % i, [128, 1024], BF16)[0] for i in range(2)]
    xws = [sb("xw%d" % i, [128, 1024], BF16)[0] for i in range(2)]
    Btms = [sb("Btm%d" % i, [128, 512], BF16)[0] for i in range(2)]
    exbs = [sb("exb%d" % i, [128, 48], F32)[0] for i in range(2)]
    xDs = [sb("xD%d" % i, [128, 1024], BF16)[0] for i in range(2)]
    RC2 = cur[0]
    cur[0] = RC
    mrg, _, _ = sb("mrg", [128, 8, 512], BF16)
    msb, _, _ = sb("msb", [128, 8, 512], F32)
    sg, _, _ = sb("sg", [128, 3, 512], F32)
    RC3 = cur[0]
    cur[0] = max(RC1, RC2, RC3)
    TOT = cur[0]
    assert TOT - (RC + 24 * 1024) >= 16 * 1024
    xpre = nc.alloc_sbuf_tensor_at("xpre", [128, 8, 512], F32, offset=RC + 24 * 1024)
    cur[0] = A0
    NSTG = 4
    stg32 = [sb("stg32_%d" % i, [128, 3072], F32)[0] for i in range(NSTG)]
    stgbf = [sb("stgbf_%d" % i, [128, 3072], BF16)[0] for i in range(NSTG)]
    sst, _, _ = sb("sst", [128, 6400], F32)
    assert cur[0] <= TOT, (cur[0], TOT)
    assert TOT <= 229376 - 64, TOT
    print('SBUF bytes/partition used', TOT)

    ps = nc.alloc_psum_tensor("ps", [128, 8, 512], F32)
    bank_rr = [0]

    nbank = [8]

    def bank(n=1):
        b = bank_rr[0] % nbank[0]
        if n == 2 and b % 2:
            b = (b + 1) % nbank[0]
        bank_rr[0] = (b + n) % nbank[0]
        return b

    def mm(out, lhsT, rhs, start=True, stop=True):
        P.op("pe", lambda e: e.matmul(out, lhsT=lhsT, rhs=rhs, start=start, stop=stop), [lhsT, rhs], [out])

    def tr(out, in_, ident):
        P.op("pe", lambda e: e.transpose(out=out, in_=in_, identity=ident), [in_, ident], [out])

    def act(out, in_, func, bias=None, scale=None):
        kw = {}
        reads = [in_]
        if bias is not None:
            kw["bias"] = bias
            if not isinstance(bias, float):
                reads.append(bias)
        if scale is not None:
            kw["scale"] = scale
            if not isinstance(scale, float):
                reads.append(scale)
        P.op("act", lambda e: e.activation(out=out, in_=in_, func=func, **kw), reads, [out])

    def tt(eng, out, in0, in1, op):
        P.op(eng, lambda e: e.tensor_tensor(out=out, in0=in0, in1=in1, op=op), [in0, in1], [out])

    def ts(eng, out, in0, s1, s2, op0, op1):
        reads = [in0] + [x for x in (s1, s2) if x is not None and not isinstance(x, (float, int))]
        P.op(eng, lambda e: e.tensor_scalar(out=out, in0=in0, scalar1=s1, scalar2=s2, op0=op0, op1=op1), reads, [out])

    def stt(eng, out, in0, scalar, in1, op0, op1):
        reads = [in0, in1] + ([] if isinstance(scalar, (float, int)) else [scalar])
        P.op(eng, lambda e: e.scalar_tensor_tensor(out=out, in0=in0, scalar=scalar, in1=in1, op0=op0, op1=op1), reads, [out])

    def rsq(out, in_, scale, eps, post=1.0):
        act(out, in_, AF.Ln, bias=float(eps), scale=float(scale))
        if post == 1.0:
            act(out, out, AF.Exp, scale=-0.5)
        else:
            act(out, out, AF.Exp, scale=-0.5, bias=float(np.log(post)))

    def cp(eng, out, in_):
        if eng == "act":
            act(out, in_, AF.Copy)
        else:
            P.op(eng, lambda e: e.tensor_copy(out=out, in_=in_), [in_], [out])

    def mset(eng, out, val):
        P.op(eng, lambda e: e.memset(out, val), [], [out])

    def dma(q, out, in_, rk=None, wk=None, slow=False):
        reads = [rk if rk is not None else in_]
        writes = [wk if wk is not None else out]
        if isinstance(rk, list):
            reads = rk
        if slow:
            P.dma(q, lambda e: e.dma_start(out=out, in_=in_, allow_slow_non_contiguous=True), reads, writes)
        else:
            P.dma(q, lambda e: e.dma_start(out=out, in_=in_), reads, writes)

    inst = [0]

    class _Vec:
        def __getitem__(self, key):
            rows, cols = key
            lo, hi = cols.start, cols.stop
            if hi <= 56:
                return vecG[inst[0] % 2][rows, lo:hi]
            assert lo >= 56
            return vecM[rows, lo - 56:hi - 56]
    vec = _Vec()

    def load_vec_g(l_next):
        dma("sp", vecG[(inst[0] + 1) % 2][:, :], vecs[l_next, :, 0:56], rk=("in", "vecs"))

    def load_vec_m(l_next):
        dma("sp", vecM[:, :], vecs[l_next, :, 56:NV], rk=("in", "vecs"))

    identf, U2f, U1f, onesf = cf[:, 0, :], cf[:, 1, :], cf[:, 2, :], cf[:, 3, :]
    identb, onesb, U1b = cb[:, 0, :], cb[:, 1, :], cb[:, 14, :]

    def band(kind, g):
        return cb[:, 2 + kind * 4 + g, :]

    st0 = sst
    dma("sp", st0[:, 0:17 * 128], cst.rearrange("p a b -> p (a b)"), rk=("in", "cst"))
    cp("dve", cf[:, :, :], st0[:, 0:4 * 128].rearrange("p (a b) -> p a b", b=128))
    cp("dve", cb[:, 0, :], st0[:, 0:128])
    cp("dve", cb[:, 1, :], st0[:, 3 * 128:4 * 128])
    cp("dve", cb[:, 2:14, :], st0[:, 5 * 128:17 * 128].rearrange("p (a b) -> p a b", b=128))
    cp("dve", cb[:, 14, :], st0[:, 2 * 128:3 * 128])
    LTf = st0[:, 4 * 128:5 * 128]
    pwv = sst[:, 2176:2176 + DEPTH * 512].rearrange("p (a b) -> p a b", b=128)
    dma("sp", pwv, pool_w.rearrange("l g c d -> c (l g) d"), rk=("in", "pool_w"))
    cp("act", plw[:, :, :], pwv)
    wsv = sst[:, 4224:4224 + DEPTH * 512].rearrange("p (a b) -> p a b", b=128)
    dma("sp", wsv, ws_in.rearrange("l h t s -> t (l h) s"), rk=("in", "ws"))
    tt("dve", wsv, wsv, LTf.unsqueeze(1).to_broadcast([128, DEPTH * 4, 128]), ALU.mult)
    for i in range(DEPTH * 4):
        b = bank()
        tr(ps[:, b, 0:128], wsv[:, i, :], identf)
        cp("act", wsmT[:, i, :], ps[:, b, 0:128])

    nst = {}
    jobn = [0]
    ceng = ["act", "dve"]

    gfold = sst[:, 6280:6280 + DEPTH * 8].rearrange("p (l k) -> p l k", k=8)
    dma("sp", gfold, vecs[:, :, 48:56].rearrange("l p k -> p l k"), rk=("in", "vecs"), slow=True)

    def precast(src, W, stores, name, scale=None):
        i = jobn[0] % NSTG
        jobn[0] += 1
        dma("sp", stg32[i][:, 0:W], src, rk=("in", name))
        if scale is not None:
            act(stgbf[i][:, 0:W], stg32[i][:, 0:W], AF.Copy, scale=scale)
        else:
            cp(ceng[jobn[0] % 2], stgbf[i][:, 0:W], stg32[i][:, 0:W])
        for (dst, sview) in stores:
            k = nst.get(name, 0)
            nst[name] = k + 1
            dma("act", dst, sview, wk=("scr", name, k), slow=True)

    def sv(i_, lo, n, inner):
        return stgbf[i_][:, lo:lo + n * inner].rearrange("p (a b) -> p a b", b=inner)

    for l in range(DEPTH):
        for f in range(2):
            nm = "gu%d_%d" % (f, l)
            for kc in range(8):
                for half in range(2):
                    i_ = jobn[0] % NSTG
                    precast(w_gu[f][l, kc * 128:(kc + 1) * 128, half * DFF:(half + 1) * DFF], DFF,
                            [(gu_s[f][l, :, :, kc, half * 128:(half + 1) * 128].rearrange("j p c -> p j c"),
                              sv(i_, 0, 22, 128))], nm)
            nm = "dn%d_%d" % (f, l)
            for kc in range(22):
                i_ = jobn[0] % NSTG
                precast(w_dn[f][l, kc * 128:(kc + 1) * 128, :], D,
                        [(dn_s[f][l, :, :, kc, :].rearrange("c p d -> p c d"), sv(i_, 0, 8, 128))], nm)
        for kc in range(8):
            rows = slice(kc * 128, (kc + 1) * 128)
            q, r = kc // 2, kc % 2
            i_ = jobn[0] % NSTG
            precast(w_in[l, rows, 0:512], 512, [(wtm_s[l, q, :, r, 0:512], stgbf[i_][:, 0:512])], "wtm_%d" % l)
            i_ = jobn[0] % NSTG
            precast(w_in[l, rows, 512:3584], 3072,
                    [(wfm_s[l, 0:12, :, kc, :].rearrange("q p c -> p q c"), sv(i_, 0, 12, 256))], "wfm_%d" % l)
            i_ = jobn[0] % NSTG
            precast(w_in[l, rows, 3584:3600], 16, [(wtm_s[l, q, :, r, 1024:1040], stgbf[i_][:, 0:16])], "wtm_%d" % l)
            i_ = jobn[0] % NSTG
            precast(w_in[l, rows, 3600:4112], 512, [(wfm_s[l, 12:14, :, kc, :].rearrange("q p c -> p q c"), sv(i_, 0, 2, 256))], "wfm_%d" % l)
            i_ = jobn[0] % NSTG
            precast(w_in[l, rows, 4112:4624], 512, [(wtm_s[l, q, :, r, 512:1024], stgbf[i_][:, 0:512])], "wtm_%d" % l)
            i_ = jobn[0] % NSTG
            precast(w_in[l, rows, 4624:7696], 3072,
                    [(wmg_s[l, :, :, 16 + b_ * 8 + kc, :].rearrange("c p d -> p c d"), sv(i_, b_ * 1024, 8, 128))
                     for b_ in range(3)], "wmg_%d" % l)
        for (wsrc, nk, blk0) in ((w_a, 4, 0), (w_b, 8, 4), (w_c, 4, 12)):
            for kc in range(nk):
                i_ = jobn[0] % NSTG
                precast(wsrc[l, kc * 128:(kc + 1) * 128, :], D,
                        [(wmg_s[l, :, :, blk0 + kc, :].rearrange("c p d -> p c d"), sv(i_, 0, 8, 128))], "wmg_%d" % l,
                        scale=(sst[:, 6280 + l * 8 + kc:6280 + l * 8 + kc + 1] if blk0 == 4 else None))
        for kc in range(8):
            i_ = jobn[0] % NSTG
            precast(w_o[l, kc * 128:(kc + 1) * 128, :], D,
                    [(wo_s[l, q_, :, :, kc, :].rearrange("p c d -> p c d"), sv(i_, q_ * 512, 4, 128)) for q_ in range(2)],
                    "wo_%d" % l)

    slot_rr = [0]

    def piece(src, n, name):
        s_ = slots[slot_rr[0] % NSLOT]
        slot_rr[0] += 1
        dma("sp", s_[:, 0:n], src, rk=[("scr", name, k) for k in range(nst[name])])
        return s_

    def rms_stats(T, src_chunks, sq_dst):
        b = bank()
        n = len(src_chunks)
        for i, c in enumerate(src_chunks):
            act(sq_dst(i), c, AF.Square)
            mm(ps[:, b, 0:T], onesb, sq_dst(i), start=(i == 0), stop=(i == n - 1))
        rsq(rstd[:, 0:T], ps[:, b, 0:T], 1.0 / D, EPS)

    def prenorm(T, gcol, hsrc=None):
        hsrc = h if hsrc is None else hsrc
        rms_stats(T, [hsrc[:, kc, 0:T] for kc in range(8)], lambda i: xn[:, i, 0:T])
        for kc in range(8):
            stt("dve", xn[:, kc, 0:T], hsrc[:, kc, 0:T],
                vec[:, gcol + kc:gcol + kc + 1], rstd[:, 0:T], ALU.mult, ALU.mult)

    def ffn(l, f, T, hdst=None, hsrc=None, pre_hook=None):
        hdst = h if hdst is None else hdst
        hsrc = h if hsrc is None else hsrc
        gpre = 0 if f == 0 else 32
        gpost = 8 if f == 0 else 40
        P.tag = 'ffn.pre'
        if pre_hook is not None:
            pre_hook()
        prenorm(T, gpre, hsrc)
        P.tag = 'ffn.gu'
        for j in range(22):
            w = piece(gu_s[f][l, j].rearrange("p k c -> p (k c)"), 2048, "gu%d_%d" % (f, l))
            wv = w[:, 0:2048].rearrange("p (k c) -> p k c", c=256)
            bg, bu = bank(), bank()
            for kc in range(8):
                mm(ps[:, bg, 0:T], wv[:, kc, 0:128], xn[:, kc, 0:T], kc == 0, kc == 7)
            for kc in range(8):
                mm(ps[:, bu, 0:T], wv[:, kc, 128:256], xn[:, kc, 0:T], kc == 0, kc == 7)
            act(tA[:, 0:T], ps[:, bg, 0:T], AF.Silu)
            tt("dve", hid[:, j, 0:T], tA[:, 0:T], ps[:, bu, 0:T], ALU.mult)
        P.tag = 'ffn.dn'
        act(tC[:, 0:1], onesf[:, 0:1], AF.Ln)
        bst = bank()
        for c in range(9):
            if c < 8:
                w = piece(dn_s[f][l, c].rearrange("p k d -> p (k d)"), 22 * 128, "dn%d_%d" % (f, l))
                wv = w[:, 0:22 * 128].rearrange("p (k d) -> p k d", d=128)
                b = bank()
                if b == bst:
                    b = bank()
                for kc in range(22):
                    mm(ps[:, b, 0:T], wv[:, kc, :], hid[:, kc, 0:T], kc == 0, kc == 21)
                act(fsb[:, c, 0:T], ps[:, b, 0:T], AF.Copy, scale=vec[:, gpost + c:gpost + c + 1])
                act(xn[:, c, 0:T], ps[:, b, 0:T], AF.Square)
            if c > 0:
                mm(ps[:, bst, 0:T], onesb, xn[:, c - 1, 0:T], c == 1, c == 8)
        P.tag = 'ffn.post'
        rsq(rstd[:, 0:T], ps[:, bst, 0:T], 1.0 / D, EPS, post=0.5)
        for c in range(8):
            tq = (tA, tB, tC)[c % 3]
            pe_ = "pool" if c in (2, 5) else "dve"
            tt(pe_, tq[:, 0:T], fsb[:, c, 0:T], rstd[:, 0:T], ALU.mult)
            tt(pe_, hdst[:, c, 0:T], tq[:, 0:T], hsrc[:, c, 0:T], ALU.add)

    V_PS, V_CW, V_CB, V_DS, V_DTB, V_AL, V_GG, V_GB, V_BS, V_DB = 56, 60, 124, 140, 148, 164, 180, 692, 1204, 1716

    def mixer(l, T, CS, first, last, which, nxt):
        NCH = T // CS
        if nxt is not None:
            load_vec_g(nxt)
        P.tag = 'mx.pre'
        prenorm(T, 16)
        P.tag = 'mx.tm'
        act(tC[:, 0:16], vec[:, V_AL:V_AL + 16], AF.Exp)
        wq = []
        for q in range(4):
            w = piece(wtm_s[l, q].rearrange("p k c -> p (k c)"), 2 * 1040, "wtm_%d" % l)
            wq.append(w[:, 0:2080].rearrange("p (k c) -> p k c", c=1040))
        for c in range(NCH):
            tk = slice(c * CS, (c + 1) * CS)
            bxa, bv, bdt = bank(), bank(), bank()
            for (bb, c0, cn) in ((bxa, 0, 512), (bv, 512, 512), (bdt, 1024, 16)):
                for kc in range(8):
                    mm(ps[0:CS, bb, 0:cn], xn[:, kc, tk], wq[kc // 2][:, kc % 2, c0:c0 + cn], kc == 0, kc == 7)
            cp("act", xatm[0:CS, c, :], ps[0:CS, bxa, :])
            if c == NCH - 1:
                cp("act", xalast[0:CS, :], ps[0:CS, bxa, :])
            P.op("dve", (lambda bv_=bv: lambda e: e.bn_stats(out=st6[0:CS, 0:6], in_=ps[0:CS, bv_, :]))(),
                 [ps[0:CS, bv, :]], [st6[0:CS, 0:6]])
            P.op("dve", lambda e: e.bn_aggr(out=st6[0:CS, 6:8], in_=st6[0:CS, 0:6]), [st6[0:CS, 0:6]], [st6[0:CS, 6:8]])
            rsq(st6[0:CS, 7:8], st6[0:CS, 7:8], 1.0, EPS)
            ts("dve", st6[0:CS, 6:7], st6[0:CS, 6:7], st6[0:CS, 7:8], -1.0, ALU.mult, ALU.mult)
            act(vn32[0:CS, :], ps[0:CS, bv, :], AF.Identity, bias=st6[0:CS, 6:7], scale=st6[0:CS, 7:8])
            tt("dve", vn32[0:CS, :], vn32[0:CS, :], vec[0:CS, V_GG:V_GG + 512], ALU.mult)
            tt("dve", vn32[0:CS, :], vn32[0:CS, :], vec[0:CS, V_GB:V_GB + 512], ALU.add)
            cp("act", vnb[0:CS, c, :], vn32[0:CS, :])
            if which == 1:
                dma("act", o_gv[l], vn32[0:16, :], wk=("out", "gv", l))
            tt("dve", dtb[0:CS, c, :], ps[0:CS, bdt, 0:16], vec[0:CS, V_DTB:V_DTB + 16], ALU.add)
            act(dtb[0:CS, c, :], dtb[0:CS, c, :], AF.Exp)
            act(dtb[0:CS, c, :], dtb[0:CS, c, :], AF.Ln, bias=1.0)
            stt("dve", dtab[0:CS, c, :], dtb[0:CS, c, :], -1.0, tC[0:CS, 0:16], ALU.mult, ALU.mult)
        if last:
            if which == 0:
                dma("act", o_pool[0][l], xalast[113:128, :], wk=("out", "pool", l))
            else:
                dma("act", o_pool[1][l], xalast[1:16, :], wk=("out", "pool", l))
        P.tag = 'mx.pool'
        for c in range(NCH):
            b = bank()
            for g in range(4):
                gs = slice(g * 128, (g + 1) * 128)
                mm(ps[:, b, g * 128:g * 128 + CS], xatm[0:CS, c, gs], band(0 if (first and c == 0 and which == 0) else 1, g)[0:CS, 0:CS],
                   True, False)
                if c == 0:
                    mm(ps[:, b, g * 128:g * 128 + CS], xac[l][64:128, gs], band(2, g)[64:128, 0:CS], False, True)
                else:
                    mm(ps[:, b, g * 128:g * 128 + CS], xatm[64:128, c - 1, gs], band(2, g)[64:128, 0:CS], False, True)
            cp("act", zz[:, :, c * CS:(c + 1) * CS], ps[:, b, :].rearrange("p (g t) -> p g t", t=128)[:, :, 0:CS])
        cp("pool", xac[l][:, :], xatm[:, NCH - 1, :]) if CS == 128 else None
        Mbs = [Mb, Mb2]

        def S3(c):
            tk = slice(c * CS, (c + 1) * CS)
            xdt, Btm, xD = xdts[c % 2], Btms[c % 2], xDs[c % 2]
            b = bank()
            pb = ps[:, b, :].bitcast(BF16)
            for j in range(8):
                tr(pb[0:CS, j * 128:(j + 1) * 128], xs[:, j, tk], identb)
            b2 = bank()
            pb2 = ps[:, b2, :].bitcast(BF16)
            for g in range(4):
                tr(pb2[0:CS, g * 128:(g + 1) * 128], BC[:, g, tk], identb)
            dtv = dtb[0:CS, c, :].unsqueeze(2).to_broadcast([CS, 16, 64])
            p3 = pb[0:CS, 0:1024].rearrange("p (h q) -> p h q", q=64)
            tt("dve", xdt[0:CS, :].rearrange("p (h q) -> p h q", q=64), p3, dtv, ALU.mult)
            tt("dve", xD[0:CS, :].rearrange("p (h q) -> p h q", q=64), p3,
               vec[0:CS, V_DB:V_DB + 16].unsqueeze(2).to_broadcast([CS, 16, 64]), ALU.mult)
            cp("act", Btm[0:CS, :], pb2[0:CS, 0:512])

        def S1(c):
            tk = slice(c * CS, (c + 1) * CS)
            dta = dtab[0:CS, c, :]
            exb = exbs[c % 2]
            xdt, xw = xdts[c % 2], xws[c % 2]
            b = bank()
            mm(ps[0:CS, b, 0:16], U2f[0:CS, 0:CS], dta)
            mm(ps[0:CS, b, 16:32], U1f[0:CS, 0:CS], dta)
            mm(ps[:, b, 32:48], onesf[0:CS, :], dta)
            b2 = bank()
            for g in range(4):
                mm(ps[0:CS, b2, g * 128:g * 128 + CS], BC[:, g, tk], BC[:, 4 + g, tk])
            act(exb[0:CS, 0:32], ps[0:CS, b, 0:32], AF.Exp)
            act(exb[:, 32:48], ps[:, b, 32:48], AF.Exp)
            tt("dve", dtaU[0:CS, :, 0:CS], dta.unsqueeze(2).to_broadcast([CS, 16, CS]),
               U2f[0:CS, 0:CS].unsqueeze(1).to_broadcast([CS, 16, CS]), ALU.mult)
            tt("dve", cbm[0:CS, :, 0:CS], ps[0:CS, b2, :].rearrange("p (g t) -> p g t", t=128)[:, :, 0:CS],
               U2f[0:CS, 0:CS].unsqueeze(1).to_broadcast([CS, 4, CS]), ALU.mult)
            tt("pool", xw[0:CS, :].rearrange("p (h q) -> p h q", q=64), xdt[0:CS, :].rearrange("p (h q) -> p h q", q=64),
               exb[0:CS, 16:32].unsqueeze(2).to_broadcast([CS, 16, 64]), ALU.mult)

        def S2(c):
            Mc = Mbs[c % 2]
            for hf in range(2):
                b = bank(2)
                for q4 in range(2):
                    hq = hf * 2 + q4
                    if CS == 128:
                        mm(ps[0:CS, b + q4, :], U1b[0:CS, 0:CS], dtaU[0:CS, hq * 4:(hq + 1) * 4, :].rearrange("p a b -> p (a b)"))
                    else:
                        for hh in range(4):
                            mm(ps[0:CS, b + q4, hh * 128:hh * 128 + CS], U1b[0:CS, 0:CS], dtaU[0:CS, hq * 4 + hh, 0:CS])
                act(Eb[0:CS, hf * 8:(hf + 1) * 8, 0:CS],
                    ps[0:CS, b:b + 2, :].rearrange("p a (h t) -> p (a h) t", t=128)[:, :, 0:CS], AF.Exp)
            tt("dve", Mc[0:CS, :, 0:CS].rearrange("p (g r) t -> p g r t", r=4),
               Eb[0:CS, :, 0:CS].rearrange("p (g r) t -> p g r t", r=4),
               cbm[0:CS, :, 0:CS].unsqueeze(2).to_broadcast([CS, 4, 4, CS]), ALU.mult)

        stb = {}

        def S4(c):
            tk = slice(c * CS, (c + 1) * CS)
            exb = exbs[c % 2]
            Mc, xdt, xw, Btm, xD = Mbs[c % 2], xdts[c % 2], xws[c % 2], Btms[c % 2], xDs[c % 2]
            ecum = exb[0:CS, 0:16]
            by = bank(2)
            for hb in range(2):
                mm(ps[0:CS, by + hb, :], identb[0:CS, 0:CS], xD[0:CS, hb * 512:(hb + 1) * 512], True, False)
            for hh in range(16):
                mm(ps[0:CS, by + hh // 8, (hh % 8) * 64:(hh % 8) * 64 + 64], Mc[0:CS, hh, 0:CS], xdt[0:CS, hh * 64:(hh + 1) * 64],
                   False, hh % 8 == 7)
            bo = bank(2)
            for g in range(4):
                mm(ps[0:CS, bo + g // 2, (g % 2) * 256:(g % 2) * 256 + 256], BC[:, 4 + g, tk], Sbf1[:, g * 256:(g + 1) * 256])
            bs_ = 6
            for g in range(4):
                mm(ps[:, bs_ + g // 2, (g % 2) * 256:(g % 2) * 256 + 256], Btm[0:CS, g * 128:(g + 1) * 128],
                   xw[0:CS, g * 256:(g + 1) * 256])
            stb[c] = bs_
            pyo = ps[0:CS, bo:bo + 2, :].rearrange("p a (h q) -> p (a h) q", q=64)
            pyd = ps[0:CS, by:by + 2, :].rearrange("p a (h q) -> p (a h) q", q=64)
            y3 = ytm[0:CS, :].rearrange("p (h q) -> p h q", q=64)
            tt("dve", y3, pyo, ecum.unsqueeze(2).to_broadcast([CS, 16, 64]), ALU.mult)
            tt("dve", y3, y3, pyd, ALU.add)

        def S5(c):
            tk = slice(c * CS, (c + 1) * CS)
            byt = bank(2)
            for j in range(8):
                tr(ps[:, byt + j // 4, (j % 4) * 128:(j % 4) * 128 + CS], ytm[0:CS, j * 128:(j + 1) * 128], identf[0:CS, 0:CS])
            pyt = ps[:, byt:byt + 2, :].rearrange("p a (j t) -> p (a j) t", t=128)[:, :, 0:CS]
            elast = exbs[c % 2][:, 32:48]
            s3 = S32[l][:, :].rearrange("p (h q) -> p h q", q=64)
            tt("pool", s3, s3, elast.unsqueeze(2).to_broadcast([128, 16, 64]), ALU.mult)
            tt("dve", ygb[:, :, 0:CS], pyt, sz[:, :, tk], ALU.mult)
            act(sqb[:, :, 0:CS], ygb[:, :, 0:CS], AF.Square)
            bs_ = stb[c]
            tt("dve", S32[l][:, :].rearrange("p (a n) -> p a n", a=2), S32[l][:, :].rearrange("p (a n) -> p a n", a=2),
               ps[:, bs_:bs_ + 2, :], ALU.add)
            cp("act", Sbf1[:, :], S32[l][:, :])

        def S6(c):
            tk = slice(c * CS, (c + 1) * CS)
            elast = exbs[c % 2][:, 32:48]
            b = bank()
            for j in range(8):
                mm(ps[:, b, 0:CS], onesb, sqb[:, j, 0:CS], j == 0, j == 7)
            rsq(rsb[:, 0:CS], ps[:, b, 0:CS], 1.0 / D, EPS)
            tt("dve", yb[:, :, tk], ygb[:, :, 0:CS], rsb[:, 0:CS].unsqueeze(1).to_broadcast([128, 8, CS]), ALU.mult)

        ssd_pre = [None]

        def _ssd_pre():
            cp("act", Sbf1[:, :], S32[l][:, :])
            nbank[0] = 6
            S3(0)
            S1(0)
        ssd_pre[0] = _ssd_pre
        P.tag = 'mx.fm'
        accs = [tA, tB, tC]
        pend = []
        for pi in (12, 13, 4, 5, 6, 7, 8, 9, 10, 11, 0, 1, 2, 3):
            w = piece(wfm_s[l, pi].rearrange("p k c -> p (k c)"), 2048, "wfm_%d" % l)
            wv = w[:, 0:2048].rearrange("p (k c) -> p k c", c=256)
            for ci in range(2):
                b = bank()
                for kc in range(8):
                    mm(ps[:, b, 0:T], wv[:, kc, ci * 128:(ci + 1) * 128], xn[:, kc, 0:T], kc == 0, kc == 7)
                if pi >= 12:
                    cp("act", ubuf[:, (pi - 12) * 2 + ci, 0:T], ps[:, b, 0:T])
                elif pi < 4:
                    act(sz[:, pi * 2 + ci, 0:T], ps[:, b, 0:T], AF.Silu)
                else:
                    j = (pi - 4) * 2 + ci
                    ext = exts[j % 3]
                    acc = accs[j % 3]
                    cp("pool", ext[:, 0:3], hist[l][:, j, :])
                    cp("act", ext[:, 3:3 + T], ps[:, b, 0:T])
                    cp("pool", hist[l][:, j, :], ext[:, T:T + 3])
                    cw = V_CW + j * 4
                    act(acc[:, 0:T], ext[:, 0:T], AF.Identity, bias=vec[:, V_CB + j:V_CB + j + 1], scale=vec[:, cw:cw + 1])
                    stt("dve", acc[:, 0:T], ext[:, 1:1 + T], vec[:, cw + 1:cw + 2], acc[:, 0:T], ALU.mult, ALU.add)
                    stt("dve", acc[:, 0:T], ext[:, 2:2 + T], vec[:, cw + 2:cw + 3], acc[:, 0:T], ALU.mult, ALU.add)
                    stt("dve", acc[:, 0:T], ext[:, 3:3 + T], vec[:, cw + 3:cw + 4], acc[:, 0:T], ALU.mult, ALU.add)
                    dst = xs[:, j, 0:T] if j < 8 else BC[:, j - 8, 0:T]
                    for (d_, a_) in pend:
                        act(d_, a_, AF.Silu)
                    pend = [(dst, acc[:, 0:T])]
        for (d_, a_) in pend:
            act(d_, a_, AF.Silu)
        for g in range(4):
            b = bank()
            mm(ps[:, b, 0:T], plw[:, l * 4 + g, :], zz[:, g, 0:T])
            act(ya[:, g, 0:T], ps[:, b, 0:T], AF.Copy, scale=vec[:, V_PS + g:V_PS + g + 1])
        ssd_pre[0]()
        for c in range(NCH):
            b = bank()
            for hd in range(4):
                mm(ps[:, b, hd * 128:hd * 128 + CS], vnb[0:CS, c, hd * 128:(hd + 1) * 128],
                   wsmT[0:CS, l * 4 + hd, 0:CS])
            pv = ps[:, b, :].rearrange("p (g t) -> p g t", t=128)[:, :, 0:CS]
            tv = tA[:, 0:512].rearrange("p (g t) -> p g t", t=128)[:, :, 0:CS]
            tt("dve", tv, pv, vec[:, V_BS:V_BS + 512].rearrange("p (g t) -> p g t", t=128)[:, :, 0:CS], ALU.add)
            tt("dve", yc[:, :, c * CS:(c + 1) * CS], tv, ubuf[:, :, c * CS:(c + 1) * CS], ALU.mult)
        P.tag = 'mx.ssd'
        S2(0)
        for c in range(NCH):
            S4(c)
            if c + 1 < NCH:
                S3(c + 1)
                S1(c + 1)
            S5(c)
            if c + 1 < NCH:
                S2(c + 1)
            S6(c)
        nbank[0] = 8
        if CS < 128:
            pass
        if last:
            dma("act", o_conv[which][l], hist[l][:, :, :], wk=("out", "conv", l))
            dma("act", o_ssm[which][l], S32[l][:, :], wk=("out", "ssm", l))
        P.tag = 'mx.merge'
        for c in range(8):
            w0 = piece(wmg_s[l, c, :, 0:20, :].rearrange("p k d -> p (k d)"), 2560, "wmg_%d" % l)
            w1 = piece(wmg_s[l, c, :, 20:40, :].rearrange("p k d -> p (k d)"), 2560, "wmg_%d" % l)
            wv0 = w0[:, 0:2560].rearrange("p (k d) -> p k d", d=128)
            wv1 = w1[:, 0:2560].rearrange("p (k d) -> p k d", d=128)

            def wvk(k_):
                return wv0[:, k_, :] if k_ < 20 else wv1[:, k_ - 20, :]
            sset = (sg[:, 0, 0:T], sg[:, 1, 0:T], sg[:, 2, 0:T]) if c % 2 == 0 else (tA[:, 0:T], tB[:, 0:T], tC[:, 0:T])
            for i_, (nk, src, blk0) in enumerate(((4, ya, 0), (8, yb, 4), (4, yc, 12))):
                bg = bank()
                for kc in range(8):
                    mm(ps[:, bg, 0:T], wvk(16 + i_ * 8 + kc), xn[:, kc, 0:T], kc == 0, kc == 7)
                act(sset[i_], ps[:, bg, 0:T], AF.Sigmoid)
                bb = bank()
                for kc in range(nk):
                    mm(ps[:, bb, 0:T], wvk(blk0 + kc), src[:, kc, 0:T], kc == 0, kc == nk - 1)
                tt("dve", sset[i_], sset[i_], ps[:, bb, 0:T], ALU.mult)
            tt("pool", sset[0], sset[0], sset[1], ALU.add)
            tt("pool", mrg[:, c, 0:T], sset[0], sset[2], ALU.add)
        P.tag = 'mx.out'
        if nxt is not None:
            load_vec_m(nxt)
        wos = []
        for q in range(4):
            w = piece(wo_s[l, q // 2, :, (q % 2) * 2:(q % 2) * 2 + 2].rearrange("p c k d -> p (c k d)"), 2048, "wo_%d" % l)
            wos.append(w[:, 0:2048].rearrange("p (c k d) -> p c k d", k=8, d=128))
        act(tC[:, 0:1], onesf[:, 0:1], AF.Ln)
        bst = bank()
        for c in range(9):
            if c < 8:
                b = bank()
                if b == bst:
                    b = bank()
                for kc in range(8):
                    mm(ps[:, b, 0:T], wos[c // 2][:, c % 2, kc, :], mrg[:, kc, 0:T], kc == 0, kc == 7)
                act(msb[:, c, 0:T], ps[:, b, 0:T], AF.Copy, scale=vec[:, 24 + c:24 + c + 1])
                act(xn[:, c, 0:T], ps[:, b, 0:T], AF.Square)
            if c > 0:
                mm(ps[:, bst, 0:T], onesb, xn[:, c - 1, 0:T], c == 1, c == 8)
        rsq(rstd[:, 0:T], ps[:, bst, 0:T], 1.0 / D, EPS)
        for c in range(8):
            tq = (tA, tB, tC)[c % 3]
            tt("dve", tq[:, 0:T], msb[:, c, 0:T], rstd[:, 0:T], ALU.mult)
            tt("dve", h[:, c, 0:T], tq[:, 0:T], h[:, c, 0:T], ALU.add)

    def run_tile(T, CS, first, last, which, src, dst, final, pref, nxt_src):
        if not pref:
            dma("sp", h[:, :, 0:T], src, rk=("in", "x"))
        for l in range(DEPTH):
            if inst[0] == 0:
                dma("sp", vecG[0][:, :], vecs[0, :, 0:56], rk=("in", "vecs"))
                load_vec_m(0)
            nxt = None if (final and l == DEPTH - 1) else (l + 1) % DEPTH
            if first:
                if which == 0:
                    mset("dve", S32[l][:, :], 0.0)
                    mset("dve", hist[l][:, :, :], 0.0)
                    mset("dve", xac[l][:, :], 0.0)
                else:
                    dma("sp", S32[l][:, :], ssm_st[l], rk=("in", "ssm"))
                    dma("sp", hist[l][:, :, :], conv_st[l], rk=("in", "conv"))
                    mset("dve", vn32[:, :], 0.0)
                    dma("sp", vn32[113:128, :], pool_st[l], rk=("in", "poolst"))
                    cp("act", xac[l][:, :], vn32[:, :])
            ffn(l, 0, T, hsrc=(xpre if (pref and l == 0) else None))
            mixer(l, T, CS, first, last, which, nxt)
            hook = None
            if l == DEPTH - 1 and nxt_src is not None:
                def hook(ns=nxt_src):
                    dma("sp", xpre[:, :, 0:ns[1]], ns[0], rk=("in", "x"))
            ffn(l, 1, T, hdst=(msb if l == DEPTH - 1 else None), pre_hook=hook)
            inst[0] += 1
        dma("act", dst, msb[:, :, 0:T], wk=("out", "y", which, id(dst)))

    for t in range(NT):
        nsrc = (xT[t + 1], 512) if t + 1 < NT else ((xsT, 16) if has_sample else None)
        run_tile(512, 128, t == 0, t == NT - 1, 0, xT[t], yT[t], (not has_sample) and t == NT - 1, t > 0, nsrc)
    if has_sample:
        run_tile(16, 16, True, True, 1, xsT, ysT, True, True, None)

    eng_obj = {"pe": nc.tensor, "act": nc.scalar, "dve": nc.vector, "pool": nc.gpsimd, "sp": nc.sync}
    names = ["pe", "act", "dve", "pool"] + ["d%d" % i for i in range(NDS)]
    import contextlib
    with contextlib.ExitStack() as es_:
        sems = {n: es_.enter_context(nc.semaphore("s_" + n)) for n in names}
        block = es_.enter_context(nc.Block())
        fin = [("d%d" % i, P.dma_tot[i]) for i in range(NDS) if P.dma_tot[i] > 0]

        def replay(name, e, tail=False):
            for waits, fn, sk, inc in P.q[name]:
                for (wsk, v) in waits:
                    e.wait_ge(sems[wsk], v)
                fn(e).then_inc(sems[sk], inc)
            if tail:
                for (wsk, v) in fin:
                    e.wait_ge(sems[wsk], v)

        @block.sync
        def _(e):
            replay("sp", e)

        @block.tensor
        def _(e):
            replay("pe", e)

        @block.scalar
        def _(e):
            replay("act", e, tail=True)

        @block.vector
        def _(e):
            replay("dve", e)

        @block.gpsimd
        def _(e):
            replay("pool", e)
    return nc, {e: len(P.q[e]) for e in P.q}, P.tags


def make_consts():
    k = np.arange(128)[:, None]
    t = np.arange(128)[None, :]
    c = np.zeros((128, 17, 128), np.float32)
    c[:, 0] = (k == t)
    c[:, 1] = (k <= t)
    c[:, 2] = (k > t)
    c[:, 3] = 1.0
    c[:, 4] = (k >= t)
    for g, w in enumerate((2, 4, 8, 16)):
        inwin = (k <= t) & (k > t - w)
        cnt0 = np.minimum(t + 1, w).astype(np.float32)
        c[:, 5 + g] = inwin / cnt0 - (k == t)
        c[:, 9 + g] = inwin / np.float32(w) - (k == t)
        c[:, 13 + g] = ((t + 128 - k) <= (w - 1)) / np.float32(w)
    return c


def pack_vecs(inp, depth):
    v = np.zeros((depth, 128, NV), np.float32)

    def fm(a, n):
        return np.transpose(a.reshape(depth, n, 128), (0, 2, 1))
    o = 0
    for nm in ("ffn1_pre_g", "ffn1_post_g", "mix_pre_g", "mix_post_g", "ffn2_pre_g", "ffn2_post_g", "ssm_norm_g"):
        v[:, :, o:o + 8] = fm(inp[nm], 8)
        o += 8
    v[:, :, 56:60] = fm(inp["pool_scale"], 4)
    cw = inp["ssm_conv_w"].reshape(depth, 4, 16, 128)
    v[:, :, 60:124] = np.transpose(cw, (0, 3, 2, 1)).reshape(depth, 128, 64)
    v[:, :, 124:140] = fm(inp["ssm_conv_b"], 16)
    ds = inp["ssm_d"].reshape(depth, 8, 2)
    v[:, :, 140:148] = np.repeat(np.transpose(ds, (0, 2, 1)), 64, axis=1)
    v[:, :, 148:164] = np.broadcast_to(inp["ssm_dt_bias"][:, None, :], (depth, 128, 16))
    v[:, :, 164:180] = np.broadcast_to(inp["ssm_a_log"][:, None, :], (depth, 128, 16))
    v[:, :, 180:692] = np.broadcast_to(inp["gmlp_norm_g"][:, None, :], (depth, 128, 512))
    v[:, :, 692:1204] = np.broadcast_to(inp["gmlp_norm_b"][:, None, :], (depth, 128, 512))
    v[:, :, 1204:1716] = np.broadcast_to(inp["gmlp_bs"].reshape(depth, 1, 512), (depth, 128, 512))
    v[:, :, 1716:1732] = np.broadcast_to(inp["ssm_d"][:, None, :], (depth, 128, 16))
    return v


_CACHE = {}


def run(inp, n_cores, seq, depth, runner=None):
    NT = seq // 512
    key = (NT, depth)
    if key not in _CACHE:
        _CACHE[key] = build(NT, depth)
    nc = _CACHE[key][0]
    f32 = np.float32
    vecs = pack_vecs(inp, depth)
    cst = make_consts()
    shared = {
        "w_gu1": inp["ffn1_w_gu"], "w_gu2": inp["ffn2_w_gu"], "w_dn1": inp["ffn1_w_down"], "w_dn2": inp["ffn2_w_down"],
        "w_in": inp["w_in"], "w_a": inp["w_branch_a"], "w_b": inp["w_branch_b"], "w_c": inp["w_branch_c"],
        "w_o": inp["w_out"], "vecs": vecs, "pool_w": inp["pool_w"], "ws": inp["gmlp_ws"], "cst": cst,
    }
    shared = {k: np.ascontiguousarray(v, dtype=f32) for k, v in shared.items()}
    in_maps = []
    for c in range(n_cores):
        xp = inp["x_prompt"][c].reshape(NT, 512, 8, 128)
        m = dict(shared)
        m["xT"] = np.ascontiguousarray(np.transpose(xp, (0, 3, 2, 1)), dtype=f32)
        m["xsT"] = np.ascontiguousarray(np.transpose(inp["x_sample"][c].reshape(16, 8, 128), (2, 1, 0)), dtype=f32)
        m["pool_st"] = np.ascontiguousarray(inp["state_pool"][:, c], dtype=f32)
        m["conv_st"] = np.ascontiguousarray(np.transpose(inp["state_conv"][:, c].reshape(depth, 3, 16, 128), (0, 3, 2, 1)), dtype=f32)
        m["ssm_st"] = np.ascontiguousarray(np.transpose(inp["state_ssm"][:, c].reshape(depth, 1024, 128), (0, 2, 1)), dtype=f32)
        in_maps.append(m)
    if runner is None:
        res = run_bass_kernel_spmd(nc, in_maps, core_ids=list(range(n_cores))).results
    else:
        res = runner(nc, in_maps)
    B = n_cores
    y_p = np.stack([np.transpose(r["yT"], (0, 3, 2, 1)).reshape(seq, D) for r in res])
    y_s = np.stack([np.transpose(r["ysT"], (2, 1, 0)).reshape(16, D) for r in res])

    def conv_back(a):
        return np.transpose(a, (0, 3, 2, 1)).reshape(depth, 3, 2048)

    def ssm_back(a):
        return np.transpose(a, (0, 2, 1)).reshape(depth, 16, 64, 128)
    outs = [y_p, y_s]
    for sfx in ("p", "s"):
        outs.append(np.stack([r["npool_" + sfx] for r in res], axis=1))
        outs.append(np.stack([conv_back(r["nconv_" + sfx]) for r in res], axis=1))
        outs.append(np.stack([ssm_back(r["nssm_" + sfx]) for r in res], axis=1))
    outs.append(np.stack([r["gv_s"] for r in res], axis=1))
    return tuple(np.ascontiguousarray(o, dtype=f32) for o in outs)


def kernel(**inputs):
    inp = {k: np.asarray(v) for k, v in inputs.items()}
    return run(inp, 8, 8192, 4)
```
